# Optimizing a Trainium2 kernel written in Bass

```python
import jax, jax.numpy as jnp
from jax import lax
import numpy as np

D_MODEL = 2048
BATCH = 2
SEQ = 16384
DEPTH = 1
DEC_BATCH = 16
DEC_SEQ = 64
PAST_LEN = 1024

CHUNK = 64
D_CONV = 1024
CONV_WIDTH = 3
HGRN_HEADS = 16
HGRN_DK = 128
HGRN_DV = 128
D_HGRN = HGRN_HEADS * HGRN_DK
N_BRANCH = 2
D_FF = 5632
HGRN_BLOCK = 16
NORM_EPS = 1e-6
D_IN = 3 * D_CONV + 4 * D_HGRN + N_BRANCH * D_MODEL

kernel_name = "hybrid_shortconv_hgrn2_macaron_step"


def rms_norm(x, g):
    xf = x.astype(jnp.float32)
    r = lax.rsqrt(jnp.mean(xf * xf, axis=-1, keepdims=True) + NORM_EPS)
    return (xf * r * g.astype(jnp.float32)).astype(x.dtype)


def swiglu(h, w_gate, w_up, w_down):
    return (jax.nn.silu(h @ w_gate) * (h @ w_up)) @ w_down


def causal_conv(u, buf, w):
    t = u.shape[1]
    full = jnp.concatenate([buf.astype(u.dtype), u], axis=1)
    y = w[0] * full[:, 0:t]
    for j in range(1, CONV_WIDTH):
        y = y + w[j] * full[:, j:j + t]
    return y, full[:, -(CONV_WIDTH - 1):]


def hgrn2_recurrence(q, logf, k, v, s0):
    b, t, h, dk = q.shape
    dv = v.shape[-1]
    n = -(-t // HGRN_BLOCK)
    pad = n * HGRN_BLOCK - t
    padw = ((0, 0), (0, pad), (0, 0), (0, 0))

    def blocks(a):
        a = jnp.pad(a, padw)
        return a.reshape(b, n, HGRN_BLOCK, h, a.shape[-1]).transpose(1, 0, 2, 3, 4)

    mask = jnp.tril(jnp.ones((HGRN_BLOCK, HGRN_BLOCK), dtype=bool))

    def step(s, blk):
        qb, gb, kb, vb = blk
        g_cum = jnp.cumsum(gb, axis=1)
        g_last = g_cum[:, -1]
        q_dec = qb * jnp.exp(g_cum)
        k_inv = kb * jnp.exp(-g_cum)
        k_end = kb * jnp.exp(g_last[:, None] - g_cum)
        a = jnp.einsum('blhk,bshk->bhls', q_dec, k_inv)
        a = jnp.where(mask, a, 0.0)
        o = (jnp.einsum('bhls,bshv->blhv', a, vb)
             + jnp.einsum('blhk,bhkv->blhv', q_dec, s))
        s_new = jnp.exp(g_last)[..., None] * s + jnp.einsum('bshk,bshv->bhkv', k_end, vb)
        return s_new, o

    s_fin, o = lax.scan(step, s0.astype(jnp.float32),
                        (blocks(q), blocks(logf), blocks(k), blocks(v)))
    o = o.transpose(1, 0, 2, 3, 4).reshape(b, n * HGRN_BLOCK, h, dv)[:, :t]
    return o, s_fin


def token_mixing(h, lb, conv_buf, s_hgrn, w_in, conv_w, hgrn_norm, w_br_conv, w_br_hgrn, w_out):
    b, t, _ = h.shape
    proj = h @ w_in
    idx = list(np.cumsum([D_CONV, D_CONV, D_CONV, D_HGRN, D_HGRN, D_HGRN, D_HGRN, D_MODEL]))
    gb, gc, vc, q, fz, iv, og, g_conv, g_hgrn = jnp.split(proj, idx, axis=-1)

    conv_y, new_buf = causal_conv(gc * vc, conv_buf, conv_w)
    y_conv = gb * conv_y

    f = lb + (1.0 - lb) * jax.nn.sigmoid(fz.astype(jnp.float32))
    logf = jnp.log(f)
    kk = 1.0 - f
    qf = jax.nn.silu(q.astype(jnp.float32))
    heads = lambda a: a.reshape(b, t, HGRN_HEADS, -1)
    o, new_s = hgrn2_recurrence(heads(qf), heads(logf), heads(kk),
                                heads(iv.astype(jnp.float32)), s_hgrn)
    o = o * lax.rsqrt(jnp.mean(o * o, axis=-1, keepdims=True) + NORM_EPS)
    o = o * hgrn_norm.astype(jnp.float32).reshape(HGRN_HEADS, HGRN_DV)
    y_hgrn = (o.reshape(b, t, D_HGRN) * jax.nn.silu(og.astype(jnp.float32))).astype(h.dtype)

    merged = (jax.nn.sigmoid(g_conv) * (y_conv @ w_br_conv)
              + jax.nn.sigmoid(g_hgrn) * (y_hgrn @ w_br_hgrn))
    return merged @ w_out, new_buf, new_s


def trunk(x, conv_bufs, hgrn_states, w):
    lb_all = jnp.cumsum(jax.nn.softmax(w['hgrn_lb_logits'].astype(jnp.float32), axis=0), axis=0)
    new_convs, new_states = [], []
    for l in range(DEPTH):
        hn = rms_norm(x, w['norm_ffn1'][l])
        x = x + 0.5 * swiglu(hn, w['w_ffn1_gate'][l], w['w_ffn1_up'][l], w['w_ffn1_down'][l])
        hn = rms_norm(x, w['norm_mix'][l])
        m, nb, ns = token_mixing(hn, lb_all[l], conv_bufs[l], hgrn_states[l], w['w_in'][l],
                                 w['conv_w'][l], w['hgrn_norm'][l], w['w_br_conv'][l],
                                 w['w_br_hgrn'][l], w['w_out'][l])
        x = x + m
        hn = rms_norm(x, w['norm_ffn2'][l])
        x = x + 0.5 * swiglu(hn, w['w_ffn2_gate'][l], w['w_ffn2_up'][l], w['w_ffn2_down'][l])
        new_convs.append(nb)
        new_states.append(ns)
    y = rms_norm(x, w['norm_final'])
    return y, jnp.stack(new_convs), jnp.stack(new_states)


def setup_inputs(seed: int = 0) -> dict:
    key = jax.random.key(seed)
    ks = jax.random.split(key, 24)
    nrm = lambda k, shape, s: jax.random.normal(k, shape, jnp.float32) * s
    gain = lambda k, shape: 1.0 + 0.01 * jax.random.normal(k, shape, jnp.float32)
    L = DEPTH
    return {
        "x_prompt": nrm(ks[0], (BATCH, SEQ, D_MODEL), 1.0),
        "x_sample": nrm(ks[1], (DEC_BATCH, DEC_SEQ, D_MODEL), 1.0),
        "cache_conv": nrm(ks[2], (L, DEC_BATCH, CONV_WIDTH - 1, D_CONV), 1.0),
        "state_hgrn": nrm(ks[3], (L, DEC_BATCH, HGRN_HEADS, HGRN_DK, HGRN_DV), 0.5),
        "norm_ffn1": gain(ks[4], (L, D_MODEL)),
        "w_ffn1_gate": nrm(ks[5], (L, D_MODEL, D_FF), D_MODEL ** -0.5),
        "w_ffn1_up": nrm(ks[6], (L, D_MODEL, D_FF), D_MODEL ** -0.5),
        "w_ffn1_down": nrm(ks[7], (L, D_FF, D_MODEL), D_FF ** -0.5),
        "norm_mix": gain(ks[8], (L, D_MODEL)),
        "w_in": nrm(ks[9], (L, D_MODEL, D_IN), D_MODEL ** -0.5),
        "conv_w": nrm(ks[10], (L, CONV_WIDTH, D_CONV), CONV_WIDTH ** -0.5),
        "hgrn_lb_logits": nrm(ks[11], (L + 1, D_HGRN), 0.1),
        "hgrn_norm": gain(ks[12], (L, D_HGRN)),
        "w_br_conv": nrm(ks[13], (L, D_CONV, D_MODEL), D_CONV ** -0.5),
        "w_br_hgrn": nrm(ks[14], (L, D_HGRN, D_MODEL), D_HGRN ** -0.5),
        "w_out": nrm(ks[15], (L, D_MODEL, D_MODEL), D_MODEL ** -0.5),
        "norm_ffn2": gain(ks[16], (L, D_MODEL)),
        "w_ffn2_gate": nrm(ks[17], (L, D_MODEL, D_FF), D_MODEL ** -0.5),
        "w_ffn2_up": nrm(ks[18], (L, D_MODEL, D_FF), D_MODEL ** -0.5),
        "w_ffn2_down": nrm(ks[19], (L, D_FF, D_MODEL), D_FF ** -0.5),
        "norm_final": gain(ks[20], (D_MODEL,)),
    }


def reference(x_prompt, x_sample, cache_conv, state_hgrn, norm_ffn1, w_ffn1_gate, w_ffn1_up,
              w_ffn1_down, norm_mix, w_in, conv_w, hgrn_lb_logits, hgrn_norm, w_br_conv,
              w_br_hgrn, w_out, norm_ffn2, w_ffn2_gate, w_ffn2_up, w_ffn2_down, norm_final):
    w = dict(norm_ffn1=norm_ffn1, w_ffn1_gate=w_ffn1_gate, w_ffn1_up=w_ffn1_up,
             w_ffn1_down=w_ffn1_down, norm_mix=norm_mix, w_in=w_in, conv_w=conv_w,
             hgrn_lb_logits=hgrn_lb_logits, hgrn_norm=hgrn_norm, w_br_conv=w_br_conv,
             w_br_hgrn=w_br_hgrn, w_out=w_out, norm_ffn2=norm_ffn2, w_ffn2_gate=w_ffn2_gate,
             w_ffn2_up=w_ffn2_up, w_ffn2_down=w_ffn2_down, norm_final=norm_final)
    zero_conv = jnp.zeros((DEPTH, x_prompt.shape[0], CONV_WIDTH - 1, D_CONV), x_prompt.dtype)
    zero_hgrn = jnp.zeros((DEPTH, x_prompt.shape[0], HGRN_HEADS, HGRN_DK, HGRN_DV), jnp.float32)
    y_prompt, new_conv_prompt, new_hgrn_prompt = trunk(x_prompt, zero_conv, zero_hgrn, w)
    y_sample, new_conv_sample, new_hgrn_sample = trunk(x_sample, cache_conv, state_hgrn, w)
    return (y_prompt, y_sample, new_conv_prompt, new_hgrn_prompt, new_conv_sample, new_hgrn_sample)
```

```python
import numpy as np
import ml_dtypes
from contextlib import ExitStack
import concourse.bass as bass
import concourse.mybir as mybir
from concourse.bass_utils import run_bass_kernel_spmd

F32 = mybir.dt.float32
BF16 = mybir.dt.bfloat16
AF = mybir.ActivationFunctionType
ALU = mybir.AluOpType

D = 2048
DFF = 5632
DCONV = 1024
NH = 16
DIN = 15360
NCORES = 8
PCH = 4096
TP = 512
EPS = 1e-6
LB = 64
KC = D // 128
NFF = DFF // 128

N_PTILES = PCH // TP
DEBUG_STOP = None
SAME_ENGINE_SYNC = True


class Res:
    __slots__ = ("w", "r", "name", "dead")

    def __init__(self, name=""):
        self.w = None
        self.r = {}
        self.name = name
        self.dead = False


class DmaSem:
    def __init__(self, key):
        self.key = key
        self.count = 0


class Prog:
    ENG = ["pe", "act", "dve", "pool", "sp"]

    def __init__(self):
        self.stream = {e: [] for e in self.ENG}
        self.seen = {e: {} for e in self.ENG}
        self.dsems = []

    def dsem(self, name):
        s = DmaSem(len(self.dsems))
        s.name = name
        self.dsems.append(s)
        return s

    def _need(self, e, tok, waits):
        if tok is None:
            return
        if tok[0] == "E":
            f, idx = tok[1], tok[2]
            if f == e and (e == "pe" or not SAME_ENGINE_SYNC):
                return
            key = ("E", f)
        else:
            key = ("D", tok[1])
            idx = tok[2]
        if self.seen[e].get(key, -1) >= idx:
            return
        self.seen[e][key] = idx
        waits.append(tok)

    def _deps(self, e, reads, writes):
        waits = []
        for r in list(reads) + list(writes):
            assert not r.dead, "stale PSUM bank generation used: " + r.name
        for r in reads:
            self._need(e, r.w, waits)
        for w in writes:
            self._need(e, w.w, waits)
            for t in w.r.values():
                self._need(e, t, waits)
        return waits

    def op(self, e, fn, reads=(), writes=()):
        waits = self._deps(e, reads, writes)
        idx = len(self.stream[e])
        self.stream[e].append([waits, fn, False, None, 0])
        tok = ("E", e, idx)
        for r in reads:
            r.r[e] = tok
        for w in writes:
            w.w = tok
            w.r = {}
        return tok

    def dma(self, q, fn, sem, reads=(), writes=(), inc=16):
        waits = self._deps(q, reads, writes)
        sem.count += inc
        tok = ("D", sem.key, sem.count)
        self.stream[q].append([waits, fn, False, sem, inc])
        for r in reads:
            r.r[("D", sem.key)] = tok
        for w in writes:
            w.w = tok
            w.r = {}
        return tok

    def emit(self, nc, es):
        for e in self.ENG:
            for ent in self.stream[e]:
                for tok in ent[0]:
                    if tok[0] == "E":
                        self.stream[tok[1]][tok[2]][2] = True
        val = {}
        for e in self.ENG:
            c = 0
            v = []
            for ent in self.stream[e]:
                if ent[2]:
                    c += 1
                v.append(c)
            val[e] = v
        self.nflag = {e: (val[e][-1] if val[e] else 0) for e in self.ENG}
        esem = {e: es.enter_context(nc.semaphore("es_" + e)) for e in self.ENG}
        dsem = [es.enter_context(nc.semaphore("ds_%d" % s.key)) for s in self.dsems]
        blk = es.enter_context(nc.Block())
        stream = self.stream

        def replay(name, eng, final=False):
            for waits, fn, flag, dm, dinc in stream[name]:
                for tok in waits:
                    if tok[0] == "E":
                        eng.wait_ge(esem[tok[1]], val[tok[1]][tok[2]])
                    else:
                        eng.wait_ge(dsem[tok[1]], tok[2])
                inst = fn(eng)
                if dm is not None:
                    if dinc == 16:
                        inst.then_inc(dsem[dm.key], 16)
                    else:
                        inst.then_inc(dsem[dm.key])
                elif flag:
                    inst.then_inc(esem[name], 1)
            if final:
                for s in self.dsems:
                    if s.count:
                        eng.wait_ge(dsem[s.key], s.count)

        @blk.tensor
        def _(e):
            replay("pe", e)

        @blk.scalar
        def _(e):
            replay("act", e)

        @blk.vector
        def _(e):
            replay("dve", e)

        @blk.gpsimd
        def _(e):
            replay("pool", e, final=True)

        @blk.sync
        def _(e):
            replay("sp", e)


class WMat:
    def __init__(self, nc, name, src, K, N, kcs, cols):
        self.name, self.src, self.K, self.N, self.kcs, self.cols = name, src, K, N, kcs, cols
        self.ncg = N // cols
        self.nkc = K // 128
        self.nks = (self.nkc + kcs - 1) // kcs
        self.sc = nc.dram_tensor("sc_" + name, [self.ncg * self.nks, 128, kcs * cols], BF16, kind="Internal").ap()
        self.res = Res("sc_" + name)

    def kc_count(self, ks):
        return min(self.kcs, self.nkc - ks * self.kcs)

    def slab_id(self, cg, ks):
        return cg * self.nks + ks


def build_program(n_ptiles=N_PTILES, debug_stop=DEBUG_STOP):
    nc = bass.Bass("TRN2", target_bir_lowering=False)
    P = Prog()
    es = ExitStack()
    NPT = n_ptiles
    npt_alloc = PCH // TP

    def din(name, shape, dt=F32):
        return nc.dram_tensor(name, shape, dt, kind="ExternalInput").ap()

    def dout(name, shape, dt=F32):
        return nc.dram_tensor(name, shape, dt, kind="ExternalOutput").ap()

    x_s = din("x_s", [128, D])
    x_p = din("x_p", [npt_alloc * TP, D])
    halo_s = din("halo_s", [2, 128, 8, 2])
    st_s = din("st_s", [2, NH, 128, 128])
    vecs_d = din("vecs", [128, 128])
    gfin_d = din("gfin", [1, D])
    ident_d = din("ident", [128, 128], BF16)
    ones_d = din("ones", [128, 128], BF16)
    tri_d = din("tri", [64, 512])
    rmask_d = din("rmask", [128, 512])
    w_in_d = {
        "g1": din("w_ffn1_gate", [D, DFF]), "u1": din("w_ffn1_up", [D, DFF]), "d1": din("w_ffn1_down", [DFF, D]),
        "g2": din("w_ffn2_gate", [D, DFF]), "u2": din("w_ffn2_up", [D, DFF]), "d2": din("w_ffn2_down", [DFF, D]),
        "win": din("w_in", [D, DIN]), "bc": din("w_br_conv", [DCONV, D]), "bh": din("w_br_hgrn", [D, D]),
        "wo": din("w_out", [D, D]),
    }
    sel_d = din("sel", [128, 24])
    x_w = din("x_w", [TP, D])
    y_s = dout("y_s", [128, D])
    y_p = dout("y_p", [npt_alloc * TP, D])
    conv_s_o = dout("conv_s_o", [2, 128, 8, 2])
    conv_p_o = dout("conv_p_o", [128, 8, 2])
    st_s_o = dout("st_s_o", [2, NH, 128, 128])
    st_p_o = dout("st_p_o", [NH, 128, 128])

    W = {
        "g1": WMat(nc, "g1", w_in_d["g1"], D, DFF, 16, 256), "u1": WMat(nc, "u1", w_in_d["u1"], D, DFF, 16, 256),
        "d1": WMat(nc, "d1", w_in_d["d1"], DFF, D, 8, 512),
        "g2": WMat(nc, "g2", w_in_d["g2"], D, DFF, 16, 256), "u2": WMat(nc, "u2", w_in_d["u2"], D, DFF, 16, 256),
        "d2": WMat(nc, "d2", w_in_d["d2"], DFF, D, 8, 512),
        "win": WMat(nc, "win", w_in_d["win"], D, DIN, 16, 256),
        "bc": WMat(nc, "bc", w_in_d["bc"], DCONV, D, 8, 256), "bh": WMat(nc, "bh", w_in_d["bh"], D, D, 16, 256),
        "wo": WMat(nc, "wo", w_in_d["wo"], D, D, 8, 512),
    }

    def sb(name, shape, dt):
        return es.enter_context(nc.sbuf_tensor("sb_" + name, shape, dt))

    x_tm = sb("x_tm", [128, 4, D], F32)
    hn_tm = sb("hn_tm", [128, 2, D], BF16)
    hT = sb("hT", [128, KC, TP], BF16)
    big = sb("big", [128, NFF, TP], BF16)
    NSLOT = 5
    wring = sb("wring", [128, NSLOT, 4096], BF16)
    gfin = sb("gfin", [128, D], F32)
    vecs = sb("vecs", [128, 128], F32)
    ident = sb("ident", [128, 128], BF16)
    ones = sb("ones", [128, 128], BF16)
    tri = sb("tri", [64, 512], F32)
    rmask = sb("rmask", [128, 512], F32)
    small = sb("small", [128, 64], F32)
    Sm = sb("Sm", [128, NH, 128], F32)
    Stmp = sb("Stmp", [128, 2, 2, 128], F32)
    halo = sb("halo", [128, 8, 3, 2], F32)
    f32w = [sb("f32w%d" % i, [128, 516], F32) for i in range(12)]
    bfw = [sb("bfw%d" % i, [128, 512], BF16) for i in range(10)]
    ke_tm = sb("ke_tm", [64, 8, 128], BF16)
    v_tm = sb("v_tm", [64, 8, 128], BF16)
    AT_bf = sb("AT_bf", [64, 512], BF16)
    Sbf = sb("Sbf", [128, 8, 128], BF16)
    sel = sb("sel", [128, 24], F32)
    xsmall = sb("xsmall", [128, 32], F32)

    NPF = 6
    pf = [es.enter_context(nc.psum_tensor("pf%d" % i, [128, 512], F32)) for i in range(NPF)]
    pb = [es.enter_context(nc.psum_tensor("pb%d" % i, [128, 1024], BF16)) for i in range(2)]
    pf_res = [Res("pf%d" % i) for i in range(NPF)]
    pb_res = [Res("pb%d" % i) for i in range(2)]
    pf_ctr = [0]
    pb_ctr = [0]

    def _regen(lst, i):
        old = lst[i]
        new = Res(old.name)
        new.w, new.r = old.w, old.r
        old.dead = True
        lst[i] = new
        return new

    def pf_next():
        i = pf_ctr[0] % NPF
        pf_ctr[0] += 1
        return pf[i], _regen(pf_res, i)

    def pb_next():
        i = pb_ctr[0] % 2
        pb_ctr[0] += 1
        return pb[i], _regen(pb_res, i)

    R_x = [Res("x%d" % i) for i in range(4)]
    R_hn = [Res("hn%d" % i) for i in range(2)]
    R_hT = Res("hT")
    R_big = [Res("big%d" % i) for i in range(NFF)]
    R_slot = [Res("slot%d" % i) for i in range(NSLOT)]
    R_const = Res("const")
    R_small = Res("small")
    R_st = [Res("st%d" % i) for i in range(4)]
    R_Sm = [Res("Sm%d" % i) for i in range(NH)]
    R_Stmp = [[Res("Stmp%d_%d" % (i, j)) for j in range(2)] for i in range(2)]
    R_halo = Res("halo")
    R_f32w = [Res("f32w%d" % i) for i in range(12)]
    R_bfw = [Res("bfw%d" % i) for i in range(10)]
    R_ke_tm, R_v_tm, R_AT, R_Sbf = Res("ke_tm"), Res("v_tm"), Res("AT"), Res("Sbf")
    R_xsmall = Res("xsmall")

    S_slot = [P.dsem("slot%d" % i) for i in range(NSLOT)]
    S_const = P.dsem("const")
    S_x = [P.dsem("x%d" % i) for i in range(4)]
    S_y = [P.dsem("y%d" % i) for i in range(4)]
    S_misc_in = P.dsem("misc_in")
    S_misc_out = P.dsem("misc_out")
    S_sti = [P.dsem("sti%d" % i) for i in range(2)]
    S_sto = [P.dsem("sto%d" % i) for i in range(2)]
    S_cso = P.dsem("cso")

    def ld_const(dst, src):
        P.dma("pool", lambda e, d=dst, s=src: e.dma_start(out=d, in_=s), S_const, writes=[R_const])

    ld_const(vecs[:], vecs_d[:, :])
    ld_const(ident[:], ident_d[:, :])
    ld_const(ones[:], ones_d[:, :])
    ld_const(tri[:], tri_d[:, :])
    ld_const(rmask[:], rmask_d[:, :])
    ld_const(sel[:], sel_d[:, :])
    ld_const(gfin[:], gfin_d.broadcast_to([128, D]))
    R_const.w = ("D", S_const.key, S_const.count)

    def load_x(tile_kind, tile_idx, nsub):
        for sub in range(nsub):
            if tile_kind == "s":
                src = x_s[:, :]
            else:
                r0 = tile_idx * TP + sub * 128
                src = x_p[r0:r0 + 128, :]
            P.dma("act", lambda e, d=x_tm[:, sub, :], s=src: e.dma_start(out=d, in_=s), S_x[sub], writes=[R_x[sub]])

    tiles = [("s", 0, 128)] + [("p", i, TP) for i in range(NPT)]
    if NPT > 0:
        for sub in range(TP // 128):
            P.dma("act", lambda e, sub=sub: e.dma_start(out=x_tm[:, sub, :], in_=x_w[sub * 128:(sub + 1) * 128, :]), S_x[sub], writes=[R_x[sub]])

    for sq in range(2):
        P.dma("pool", lambda e, d=halo[:, :, sq, :], s=halo_s[sq]: e.dma_start(out=d, in_=s), S_misc_in, writes=[R_halo])
    R_halo.w = ("D", S_misc_in.key, S_misc_in.count)

    def cast_group(key, sids, sem):
        wm = W[key]
        for sid in sids:
            cg, ks = sid // wm.nks, sid % wm.nks
            nk = wm.kc_count(ks)
            for k0 in range(0, nk, 4):
                k1 = min(nk, k0 + 4)
                r0 = (ks * wm.kcs + k0) * 128
                r1 = (ks * wm.kcs + k1) * 128
                src = wm.src[r0:r1, cg * wm.cols:(cg + 1) * wm.cols].rearrange("(kc p) c -> p kc c", p=128)
                dst = wm.sc[sid].rearrange("p (kc c) -> p kc c", c=wm.cols)[:, k0:k1, :]
                P.dma("pool", lambda e, d=dst, s=src: e.dma_start(out=d, in_=s), sem)
        r = Res("cast")
        r.w = ("D", sem.key, sem.count)
        for sid in sids:
            wm.res_of[sid] = r

    for wm in W.values():
        wm.res_of = {}
    qs4 = [range(0, 6), range(6, 12), range(12, 17), range(17, 22)]
    for gi in range(4):
        cast_group("g1", qs4[gi], P.dsem("c_g1_%d" % gi))
        cast_group("u1", qs4[gi], P.dsem("c_u1_%d" % gi))
    for cg in range(4):
        cast_group("d1", range(cg * 6, cg * 6 + 6), P.dsem("c_d1_%d" % cg))
    for gi, sids in enumerate([range(20, 28), range(28, 36), range(4, 12), list(range(0, 4)) + list(range(12, 20)), range(36, 44), range(44, 60)]):
        cast_group("win", sids, P.dsem("c_win_%d" % gi))
    for key in ["bc", "bh", "wo", "g2", "u2", "d2"]:
        wmk = W[key]
        cast_group(key, range(wmk.ncg * wmk.nks), P.dsem("c_" + key))

    ring_ctr = [0]

    def wget(key, cg, ks):
        wm = W[key]
        i = ring_ctr[0] % NSLOT
        ring_ctr[0] += 1
        nk = wm.kc_count(ks)
        n = nk * wm.cols
        sid = wm.slab_id(cg, ks)
        dst = wring[:, i, 0:n]
        src = wm.sc[sid][:, 0:n]
        P.dma("sp", lambda e, d=dst, s=src: e.dma_start(out=d, in_=s), S_slot[i], reads=[wm.res_of[sid]], writes=[R_slot[i]])
        view = wring[:, i, 0:n].rearrange("p (kc c) -> p kc c", c=wm.cols)
        return view, R_slot[i]

    vcol = lambda c0, n=1: vecs[:, c0:c0 + n]
    V_G1, V_GM, V_G2, V_HN, V_L0, V_L1, V_CW = 0, 16, 32, 48, 64, 80, 96
    SM_SSQ, SM_LN, SM_RS, SM_LB, SM_OMLB = 0, 4, 8, 16, 32

    P.op("dve", lambda e: e.tensor_tensor(out=small[:, SM_LB:SM_LB + 16], in0=vcol(V_L0, 16), in1=vcol(V_L1, 16), op=ALU.subtract),
         reads=[R_const], writes=[R_small])
    P.op("act", lambda e: e.activation(out=small[:, SM_LB:SM_LB + 16], in_=small[:, SM_LB:SM_LB + 16], func=AF.Sigmoid),
         reads=[R_small], writes=[R_small])
    P.op("dve", lambda e: e.tensor_scalar(out=small[:, SM_OMLB:SM_OMLB + 16], in0=small[:, SM_LB:SM_LB + 16], scalar1=-1.0, scalar2=1.0,
                                          op0=ALU.mult, op1=ALU.add), reads=[R_small], writes=[R_small])
    R_lb = Res("lb")
    R_lb.w = R_small.w
    P.op("dve", lambda e: e.memset(Sm[:].rearrange("p h v -> p (h v)"), 0.0), writes=R_Sm)

    def emit_norm(T, gcol0):
        nsub = T // 128

        def stats(sub):
            hs = sub % 2
            P.op("act", lambda e: e.activation(out=hn_tm[:, hs, :], in_=x_tm[:, sub, :], func=AF.Square,
                                               accum_out=small[:, SM_SSQ + sub:SM_SSQ + sub + 1]),
                 reads=[R_x[sub]], writes=[R_hn[hs], R_st[sub]])
            P.op("act", lambda e: e.activation(out=small[:, SM_LN + sub:SM_LN + sub + 1], in_=small[:, SM_SSQ + sub:SM_SSQ + sub + 1],
                                               func=AF.Ln, scale=1.0 / D, bias=eps_col[:, 0:1]), reads=[R_st[sub], R_eps], writes=[R_st[sub]])
            P.op("act", lambda e: e.activation(out=small[:, SM_RS + sub:SM_RS + sub + 1], in_=small[:, SM_LN + sub:SM_LN + sub + 1],
                                               func=AF.Exp, scale=-0.5), reads=[R_st[sub]], writes=[R_st[sub]])
            P.op("dve", lambda e: e.tensor_scalar(out=hn_tm[:, hs, :], in0=x_tm[:, sub, :], scalar1=small[:, SM_RS + sub:SM_RS + sub + 1],
                                                  scalar2=None, op0=ALU.mult), reads=[R_x[sub], R_st[sub]], writes=[R_hn[hs]])

        def xpose(sub):
            hs = sub % 2
            for half in range(2):
                bank, bres = pb_next()
                for k8 in range(8):
                    kc = half * 8 + k8
                    P.op("pe", lambda e, k8=k8, kc=kc, bank=bank: e.transpose(out=bank[:, k8 * 128:(k8 + 1) * 128], in_=hn_tm[:, hs, kc * 128:(kc + 1) * 128],
                                                                              identity=ident[:]), reads=[R_hn[hs], R_const], writes=[bres])
                for k8 in range(8):
                    kc = half * 8 + k8
                    if k8 % 2 == 0:
                        P.op("act", lambda e, k8=k8, kc=kc, bank=bank: e.activation(out=hT[:, kc, sub * 128:(sub + 1) * 128], in_=bank[:, k8 * 128:(k8 + 1) * 128],
                                                                         func=AF.Copy, scale=vecs[:, gcol0 + kc:gcol0 + kc + 1]),
                             reads=[bres, R_const], writes=[R_hT])
                    else:
                        P.op("dve", lambda e, k8=k8, kc=kc, bank=bank: e.tensor_scalar(out=hT[:, kc, sub * 128:(sub + 1) * 128], in0=bank[:, k8 * 128:(k8 + 1) * 128],
                                                                            scalar1=vecs[:, gcol0 + kc:gcol0 + kc + 1], scalar2=None, op0=ALU.mult),
                             reads=[bres, R_const], writes=[R_hT])

        stats(0)
        for sub in range(nsub):
            if sub + 1 < nsub:
                stats(sub + 1)
            xpose(sub)

    def emit_ffn(T, kg, ku, kd):
        nsub = T // 128
        for s in range(DFF // 256):
            gv, gr = wget(kg, s, 0)
            uv, ur = wget(ku, s, 0)
            for c in range(2):
                j = 2 * s + c
                bg, bgr = pf_next()
                bu, bur = pf_next()
                for kc in range(KC):
                    P.op("pe", lambda e, bg=bg, gv=gv, kc=kc, c=c: e.matmul(bg[:, 0:T], lhsT=gv[:, kc, c * 128:(c + 1) * 128], rhs=hT[:, kc, 0:T],
                                                                            start=(kc == 0), stop=(kc == KC - 1)),
                         reads=[gr, R_hT], writes=[bgr])
                for kc in range(KC):
                    P.op("pe", lambda e, bu=bu, uv=uv, kc=kc, c=c: e.matmul(bu[:, 0:T], lhsT=uv[:, kc, c * 128:(c + 1) * 128], rhs=hT[:, kc, 0:T],
                                                                            start=(kc == 0), stop=(kc == KC - 1)),
                         reads=[ur, R_hT], writes=[bur])
                wi = j % 2
                P.op("act", lambda e, bg=bg, wi=wi: e.activation(out=f32w[wi][:, 0:T], in_=bg[:, 0:T], func=AF.Silu),
                     reads=[bgr], writes=[R_f32w[wi]])
                P.op("dve", lambda e, bu=bu, wi=wi, j=j: e.tensor_tensor(out=big[:, j, 0:T], in0=bu[:, 0:T], in1=f32w[wi][:, 0:T], op=ALU.mult),
                     reads=[bur, R_f32w[wi]], writes=[R_big[j]])
        wm = W[kd]
        for cg in range(4):
            banks = [pf_next() for _ in range(nsub)]
            for ks in range(wm.nks):
                dv, dr = wget(kd, cg, ks)
                for kk in range(wm.kc_count(ks)):
                    k = ks * wm.kcs + kk
                    for sub in range(nsub):
                        b, br = banks[sub]
                        P.op("pe", lambda e, b=b, dv=dv, kk=kk, k=k, sub=sub: e.matmul(b[:, :], lhsT=big[:, k, sub * 128:(sub + 1) * 128], rhs=dv[:, kk, :],
                                                                                       start=(k == 0), stop=(k == NFF - 1)),
                             reads=[dr, R_big[k]], writes=[br])
            for sub in range(nsub):
                b, br = banks[sub]
                P.op("dve", lambda e, b=b, sub=sub, cg=cg: e.scalar_tensor_tensor(out=x_tm[:, sub, cg * 512:(cg + 1) * 512], in0=b[:, :], scalar=0.5,
                                                                                  in1=x_tm[:, sub, cg * 512:(cg + 1) * 512], op0=ALU.mult, op1=ALU.add),
                     reads=[br, R_x[sub]], writes=[R_x[sub]])

    def proj_fm(T, view, vres, c):
        b, br = pf_next()
        for kc in range(KC):
            P.op("pe", lambda e, b=b, kc=kc: e.matmul(b[:, 0:T], lhsT=view[:, kc, c * 128:(c + 1) * 128], rhs=hT[:, kc, 0:T],
                                                      start=(kc == 0), stop=(kc == KC - 1)), reads=[vres, R_hT], writes=[br])
        return b, br

    ycT = lambda cc, T: big[:, cc, 0:T]
    yhT = lambda h, T: big[:, 8 + h, 0:T]
    mT = lambda oc, T: big[:, 24 + oc, 0:T]
    R_yc = R_big[0:8]
    R_yh = R_big[8:24]
    R_m = R_big[24:40]

    def emit_conv(T, segs, kind):
        for pr in range(4):
            cvw, cvr = wget("win", 4 + pr, 0)
            vvw, vvr = wget("win", 8 + pr, 0)
            bvw, bvr = wget("win", 0 + pr, 0)
            for c in range(2):
                cc = 2 * pr + c
                bC, bCr = proj_fm(T, cvw, cvr, c)
                bV, bVr = proj_fm(T, vvw, vvr, c)
                bB, bBr = proj_fm(T, bvw, bvr, c)
                cs, csr = f32w[2], R_f32w[2]
                ub, ubr = f32w[3], R_f32w[3]
                yv, yvr = f32w[4], R_f32w[4]
                P.op("act", lambda e, bC=bC, cs=cs: e.copy(out=cs[:, 0:T], in_=bC[:, 0:T]), reads=[bCr], writes=[csr])
                for (t0, t1, sg, o) in segs:
                    n = t1 - t0
                    P.op("act", lambda e, ub=ub, o=o, cc=cc, sg=sg: e.copy(out=ub[:, o:o + 2], in_=halo[:, cc, sg, :]),
                         reads=[R_halo], writes=[ubr])
                    P.op("dve", lambda e, ub=ub, o=o, n=n, t0=t0, t1=t1, bV=bV, cs=cs: e.tensor_tensor(out=ub[:, o + 2:o + 2 + n], in0=bV[:, t0:t1],
                                                                                                      in1=cs[:, t0:t1], op=ALU.mult),
                         reads=[bVr, csr], writes=[ubr])
                    P.op("act", lambda e, ub=ub, o=o, n=n, cc=cc, sg=sg: e.copy(out=halo[:, cc, sg, :], in_=ub[:, o + n:o + n + 2]),
                         reads=[ubr], writes=[R_halo])
                    w0 = vecs[:, V_CW + cc * 3 + 0:V_CW + cc * 3 + 1]
                    w1 = vecs[:, V_CW + cc * 3 + 1:V_CW + cc * 3 + 2]
                    w2 = vecs[:, V_CW + cc * 3 + 2:V_CW + cc * 3 + 3]
                    P.op("dve", lambda e, yv=yv, ub=ub, o=o, n=n, t0=t0, t1=t1, w0=w0: e.tensor_scalar(out=yv[:, t0:t1], in0=ub[:, o:o + n], scalar1=w0,
                                                                                                      scalar2=None, op0=ALU.mult),
                         reads=[ubr, R_const], writes=[yvr])
                    P.op("dve", lambda e, yv=yv, ub=ub, o=o, n=n, t0=t0, t1=t1, w1=w1: e.scalar_tensor_tensor(out=yv[:, t0:t1], in0=ub[:, o + 1:o + 1 + n],
                                                                                                             scalar=w1, in1=yv[:, t0:t1], op0=ALU.mult, op1=ALU.add),
                         reads=[ubr, R_const, yvr], writes=[yvr])
                    P.op("dve", lambda e, yv=yv, ub=ub, o=o, n=n, t0=t0, t1=t1, w2=w2: e.scalar_tensor_tensor(out=yv[:, t0:t1], in0=ub[:, o + 2:o + 2 + n],
                                                                                                             scalar=w2, in1=yv[:, t0:t1], op0=ALU.mult, op1=ALU.add),
                         reads=[ubr, R_const, yvr], writes=[yvr])
                P.op("dve", lambda e, yv=yv, bB=bB, cc=cc: e.tensor_tensor(out=ycT(cc, T), in0=bB[:, 0:T], in1=yv[:, 0:T], op=ALU.mult),
                     reads=[bBr, yvr], writes=[R_yc[cc]])

    def emit_hgrn(T, kind, pre=False):
        nb = T // LB
        slabs = {}
        qs, qsr = f32w[0], R_f32w[0]
        fg, fgr = f32w[1], R_f32w[1]
        lf, lfr = f32w[2], R_f32w[2]
        kk, kkr = f32w[3], R_f32w[3]
        G, Gr = f32w[4], R_f32w[4]
        enG, enGr = f32w[6], R_f32w[6]
        sqb, sqr = bfw[4], R_bfw[4]
        r1, r1r = f32w[10], R_f32w[10]
        r2, r2r = f32w[11], R_f32w[11]

        def mk(h):
            p = h % 2
            c = {"h": h, "c": h % 2}
            c["eG"], c["eGr"] = (f32w[5], R_f32w[5]) if p == 0 else (f32w[8], R_f32w[8])
            c["so"], c["sor"] = (f32w[7], R_f32w[7]) if p == 0 else (f32w[9], R_f32w[9])
            for i, nm in enumerate(["vT", "qd", "ki", "ke"]):
                c[nm], c[nm + "r"] = bfw[i + 5 * p], R_bfw[i + 5 * p]
            return c

        def projFI(c):
            h = c["h"]
            if h % 2 == 0:
                hp = h // 2
                slabs["f"] = wget("win", 20 + hp, 0)
                slabs["i"] = wget("win", 28 + hp, 0)
                if not pre:
                    slabs["q"] = wget("win", 12 + hp, 0)
                    slabs["o"] = wget("win", 36 + hp, 0)
            c["bf"] = proj_fm(T, slabs["f"][0], slabs["f"][1], c["c"])
            c["bi"] = proj_fm(T, slabs["i"][0], slabs["i"][1], c["c"])

        def evacFI(c):
            bf_, bfr = c["bf"]
            bi, bir = c["bi"]
            vT, vTr = c["vT"], c["vTr"]
            P.op("act", lambda e: e.activation(out=fg[:, 0:T], in_=bf_[:, 0:T], func=AF.Sigmoid), reads=[bfr], writes=[fgr])
            P.op("act", lambda e: e.copy(out=vT[:, 0:T], in_=bi[:, 0:T]), reads=[bir], writes=[vTr])

        def projQO(c):
            if pre:
                return
            c["bq"] = proj_fm(T, slabs["q"][0], slabs["q"][1], c["c"])
            c["bo"] = proj_fm(T, slabs["o"][0], slabs["o"][1], c["c"])

        def evacQO(c):
            if pre:
                return
            bq, bqr = c["bq"]
            bo, bor = c["bo"]
            so, sor = c["so"], c["sor"]
            P.op("act", lambda e: e.activation(out=qs[:, 0:T], in_=bq[:, 0:T], func=AF.Silu), reads=[bqr], writes=[qsr])
            P.op("act", lambda e: e.activation(out=so[:, 0:T], in_=bo[:, 0:T], func=AF.Silu), reads=[bor], writes=[sor])

        def E1(c):
            h = c["h"]
            eG, eGr, ki, kir, ke, ker = c["eG"], c["eGr"], c["ki"], c["kir"], c["ke"], c["ker"]
            lbc = small[:, SM_LB + h:SM_LB + h + 1]
            omc = small[:, SM_OMLB + h:SM_OMLB + h + 1]
            P.op("dve", lambda e: e.tensor_scalar(out=fg[:, 0:T], in0=fg[:, 0:T], scalar1=omc, scalar2=lbc, op0=ALU.mult, op1=ALU.add),
                 reads=[fgr, R_lb], writes=[fgr])
            P.op("act", lambda e: e.activation(out=lf[:, 0:T], in_=fg[:, 0:T], func=AF.Ln), reads=[fgr], writes=[lfr])
            P.op("dve", lambda e: e.tensor_scalar(out=kk[:, 0:T], in0=fg[:, 0:T], scalar1=-1.0, scalar2=1.0, op0=ALU.mult, op1=ALU.add),
                 reads=[fgr], writes=[kkr])
            P.op("dve", lambda e: e.tensor_tensor_scan(out=G[:, 0:T], data0=rmask[:, 0:T], data1=lf[:, 0:T], initial=0.0, op0=ALU.mult, op1=ALU.add),
                 reads=[lfr, R_const], writes=[Gr])
            P.op("act", lambda e: e.activation(out=eG[:, 0:T], in_=G[:, 0:T], func=AF.Exp), reads=[Gr], writes=[eGr])
            P.op("act", lambda e: e.activation(out=enG[:, 0:T], in_=G[:, 0:T], func=AF.Exp, scale=-1.0), reads=[Gr], writes=[enGr])
            P.op("dve", lambda e: e.tensor_tensor(out=ki[:, 0:T], in0=kk[:, 0:T], in1=enG[:, 0:T], op=ALU.mult), reads=[kkr, enGr], writes=[kir])
            for b in range(nb):
                P.op("dve", lambda e, b=b: e.tensor_scalar(out=ke[:, b * LB:(b + 1) * LB], in0=ki[:, b * LB:(b + 1) * LB],
                                                           scalar1=eG[:, b * LB + LB - 1:b * LB + LB], scalar2=None, op0=ALU.mult),
                     reads=[kir, eGr], writes=[ker])

        def E2(c):
            if pre:
                return
            eG, eGr, qd, qdr = c["eG"], c["eGr"], c["qd"], c["qdr"]
            P.op("dve", lambda e: e.tensor_tensor(out=qd[:, 0:T], in0=qs[:, 0:T], in1=eG[:, 0:T], op=ALU.mult), reads=[qsr, eGr], writes=[qdr])

        def trans(c):
            ke, ker, vT, vTr = c["ke"], c["ker"], c["vT"], c["vTr"]
            bT1, bT1r = pb_next()
            bT2, bT2r = pb_next()
            for b in range(nb):
                P.op("pe", lambda e, b=b: e.transpose(out=bT1[0:64, b * 128:(b + 1) * 128], in_=ke[:, b * LB:(b + 1) * LB], identity=ident[:]),
                     reads=[ker, R_const], writes=[bT1r])
            for b in range(nb):
                P.op("pe", lambda e, b=b: e.transpose(out=bT2[0:64, b * 128:(b + 1) * 128], in_=vT[:, b * LB:(b + 1) * LB], identity=ident[:]),
                     reads=[vTr, R_const], writes=[bT2r])
            P.op("act", lambda e: e.copy(out=ke_tm[:, 0:nb, :].rearrange("p b d -> p (b d)"), in_=bT1[0:64, 0:nb * 128]), reads=[bT1r], writes=[R_ke_tm])
            P.op("dve", lambda e: e.tensor_copy(out=v_tm[:, 0:nb, :].rearrange("p b d -> p (b d)"), in_=bT2[0:64, 0:nb * 128]), reads=[bT2r], writes=[R_v_tm])

        def AU(c):
            ki, kir, qd, qdr = c["ki"], c["kir"], c["qd"], c["qdr"]
            if not pre:
                bA, bAr = pf_next()
                for b in range(nb):
                    P.op("pe", lambda e, b=b: e.matmul(bA[0:64, b * LB:(b + 1) * LB], lhsT=ki[:, b * LB:(b + 1) * LB], rhs=qd[:, b * LB:(b + 1) * LB],
                                                       start=True, stop=True), reads=[kir, qdr], writes=[bAr])
                P.op("dve", lambda e: e.tensor_tensor(out=AT_bf[:, 0:T], in0=bA[0:64, 0:T], in1=tri[:, 0:T], op=ALU.mult), reads=[bAr, R_const], writes=[R_AT])
            ubanks = []
            for b in range(nb):
                if b % 4 == 0:
                    ubanks.append(pf_next())
                bU, bUr = ubanks[-1]
                P.op("pe", lambda e, bU=bU, b=b: e.matmul(bU[:, (b % 4) * 128:(b % 4 + 1) * 128], lhsT=ke_tm[:, b, :], rhs=v_tm[:, b, :], start=True, stop=True),
                     reads=[R_ke_tm, R_v_tm], writes=[bUr])
            c["ubanks"] = ubanks

        def chain(c):
            h, eG, eGr, ubanks = c["h"], c["eG"], c["eGr"], c["ubanks"]
            if kind == "s":
                slot = h % 2
                for b in range(nb):
                    P.dma("pool", lambda e, b=b: e.dma_start(out=Stmp[:, slot, b, :], in_=st_s[b, h]), S_sti[slot], writes=[R_Stmp[slot][b]])
                for b in range(nb):
                    R_Stmp[slot][b].w = ("D", S_sti[slot].key, S_sti[slot].count)
                S_of = lambda b: (Stmp[:, slot, b, :], R_Stmp[slot][b])
            else:
                S_of = lambda b: (Sm[:, h, :], R_Sm[h])
            for b in range(nb):
                Sap, Sr = S_of(b)
                bU, bUr = ubanks[b // 4]
                if not pre:
                    P.op("act", lambda e, Sap=Sap, b=b: e.copy(out=Sbf[:, b, :], in_=Sap), reads=[Sr], writes=[R_Sbf])
                P.op("dve", lambda e, Sap=Sap, bU=bU, b=b: e.scalar_tensor_tensor(out=Sap, in0=Sap, scalar=eG[:, b * LB + LB - 1:b * LB + LB],
                                                                                 in1=bU[:, (b % 4) * 128:(b % 4 + 1) * 128], op0=ALU.mult, op1=ALU.add),
                     reads=[Sr, eGr, bUr], writes=[Sr])
                if kind == "s":
                    P.dma("pool", lambda e, Sap=Sap, b=b: e.dma_start(out=st_s_o[b, h], in_=Sap), S_sto[slot], reads=[Sr])

        def oT(c):
            if pre:
                return
            qd, qdr = c["qd"], c["qdr"]
            bO, bOr = pf_next()
            for b in range(nb):
                P.op("pe", lambda e, b=b: e.matmul(bO[:, b * LB:(b + 1) * LB], lhsT=v_tm[:, b, :], rhs=AT_bf[:, b * LB:(b + 1) * LB], start=True, stop=False),
                     reads=[R_v_tm, R_AT], writes=[bOr])
                P.op("pe", lambda e, b=b: e.matmul(bO[:, b * LB:(b + 1) * LB], lhsT=Sbf[:, b, :], rhs=qd[:, b * LB:(b + 1) * LB], start=False, stop=True),
                     reads=[R_Sbf, qdr], writes=[bOr])
            c["bO"] = (bO, bOr)

        def F1(c):
            if pre:
                return
            bO, bOr = c["bO"]
            P.op("act", lambda e: e.activation(out=sqb[:, 0:T], in_=bO[:, 0:T], func=AF.Square), reads=[bOr], writes=[sqr])

        def F2a(c):
            if pre or c is None:
                return
            bN, bNr = pf_next()
            c["bN"] = (bN, bNr)
            P.op("pe", lambda e: e.matmul(bN[:, 0:T], lhsT=ones[:], rhs=sqb[:, 0:T], start=True, stop=True), reads=[sqr, R_const], writes=[bNr])

        def F2b(c):
            if pre or c is None:
                return
            h, so, sor = c["h"], c["so"], c["sor"]
            bO, bOr = c["bO"]
            bN, bNr = c["bN"]
            P.op("act", lambda e: e.activation(out=r1[:, 0:T], in_=bN[:, 0:T], func=AF.Ln, scale=1.0 / 128, bias=eps_col[:, 0:1]),
                 reads=[bNr, R_eps], writes=[r1r])
            P.op("act", lambda e: e.activation(out=r1[:, 0:T], in_=r1[:, 0:T], func=AF.Exp, scale=-0.5), reads=[r1r], writes=[r1r])
            hn_c = vecs[:, V_HN + h:V_HN + h + 1]
            P.op("dve", lambda e: e.scalar_tensor_tensor(out=r2[:, 0:T], in0=bO[:, 0:T], scalar=hn_c, in1=r1[:, 0:T], op0=ALU.mult, op1=ALU.mult),
                 reads=[bOr, r1r, R_const], writes=[r2r])
            P.op("dve", lambda e: e.tensor_tensor(out=yhT(h, T), in0=r2[:, 0:T], in1=so[:, 0:T], op=ALU.mult), reads=[r2r, sor], writes=[R_yh[h]])

        c0 = mk(0)
        projFI(c0); evacFI(c0); projQO(c0); E1(c0); evacQO(c0); E2(c0)
        prev = c0
        pend = None
        for n in range(1, NH + 1):
            cn = mk(n) if n < NH else None
            if cn is not None:
                projFI(cn)
            F2a(pend)
            trans(prev)
            F2b(pend)
            pend = None
            AU(prev)
            if cn is not None:
                evacFI(cn)
                projQO(cn)
            chain(prev)
            oT(prev)
            if cn is not None:
                evacQO(cn)
                E1(cn)
            F1(prev)
            pend = prev
            if cn is not None:
                E2(cn)
            prev = cn
        F2a(pend)
        F2b(pend)

    def emit_merge(T):
        nsub = T // 128
        for pr in range(8):
            gcv, gcr = wget("win", 44 + pr, 0)
            ghv, ghr = wget("win", 52 + pr, 0)
            bcv, bcr = wget("bc", pr, 0)
            bhv, bhr = wget("bh", pr, 0)
            for c in range(2):
                oc = 2 * pr + c
                bGc, bGcr = proj_fm(T, gcv, gcr, c)
                bGh, bGhr = proj_fm(T, ghv, ghr, c)
                bPc, bPcr = pf_next()
                for k in range(8):
                    P.op("pe", lambda e, bPc=bPc, k=k, c=c, bcv=bcv: e.matmul(bPc[:, 0:T], lhsT=bcv[:, k, c * 128:(c + 1) * 128], rhs=ycT(k, T),
                                                                             start=(k == 0), stop=(k == 7)), reads=[bcr, R_yc[k]], writes=[bPcr])
                bPh, bPhr = pf_next()
                for k in range(16):
                    P.op("pe", lambda e, bPh=bPh, k=k, c=c, bhv=bhv: e.matmul(bPh[:, 0:T], lhsT=bhv[:, k, c * 128:(c + 1) * 128], rhs=yhT(k, T),
                                                                             start=(k == 0), stop=(k == 15)), reads=[bhr, R_yh[k]], writes=[bPhr])
                s1, s1r = f32w[0], R_f32w[0]
                s2, s2r = f32w[1], R_f32w[1]
                P.op("act", lambda e, s1=s1, bGc=bGc: e.activation(out=s1[:, 0:T], in_=bGc[:, 0:T], func=AF.Sigmoid), reads=[bGcr], writes=[s1r])
                P.op("act", lambda e, s2=s2, bGh=bGh: e.activation(out=s2[:, 0:T], in_=bGh[:, 0:T], func=AF.Sigmoid), reads=[bGhr], writes=[s2r])
                P.op("dve", lambda e, s1=s1, bPc=bPc: e.tensor_tensor(out=s1[:, 0:T], in0=bPc[:, 0:T], in1=s1[:, 0:T], op=ALU.mult),
                     reads=[bPcr, s1r], writes=[s1r])
                P.op("dve", lambda e, s2=s2, bPh=bPh: e.tensor_tensor(out=s2[:, 0:T], in0=bPh[:, 0:T], in1=s2[:, 0:T], op=ALU.mult),
                     reads=[bPhr, s2r], writes=[s2r])
                P.op("dve", lambda e, s1=s1, s2=s2, oc=oc: e.tensor_tensor(out=mT(oc, T), in0=s1[:, 0:T], in1=s2[:, 0:T], op=ALU.add),
                     reads=[s1r, s2r], writes=[R_m[oc]])
        wm = W["wo"]
        for cg in range(4):
            banks = [pf_next() for _ in range(nsub)]
            for ks in range(wm.nks):
                wv, wr = wget("wo", cg, ks)
                for kk in range(wm.kc_count(ks)):
                    k = ks * wm.kcs + kk
                    for sub in range(nsub):
                        b, br = banks[sub]
                        P.op("pe", lambda e, b=b, wv=wv, kk=kk, k=k, sub=sub: e.matmul(b[:, :], lhsT=big[:, 24 + k, sub * 128:(sub + 1) * 128], rhs=wv[:, kk, :],
                                                                                       start=(k == 0), stop=(k == 15)), reads=[wr, R_m[k]], writes=[br])
            for sub in range(nsub):
                b, br = banks[sub]
                P.op("dve", lambda e, b=b, sub=sub, cg=cg: e.tensor_tensor(out=x_tm[:, sub, cg * 512:(cg + 1) * 512], in0=b[:, :],
                                                                           in1=x_tm[:, sub, cg * 512:(cg + 1) * 512], op=ALU.add),
                     reads=[br, R_x[sub]], writes=[R_x[sub]])

    def emit_final(T, kind, tidx, do_norm=True):
        nsub = T // 128
        for sub in range(nsub):
            if do_norm:
                hs = sub % 2
                P.op("act", lambda e, sub=sub, hs=hs: e.activation(out=hn_tm[:, hs, :], in_=x_tm[:, sub, :], func=AF.Square,
                                                                   accum_out=small[:, SM_SSQ + sub:SM_SSQ + sub + 1]),
                     reads=[R_x[sub]], writes=[R_hn[hs], R_st[sub]])
                P.op("act", lambda e, sub=sub: e.activation(out=small[:, SM_LN + sub:SM_LN + sub + 1], in_=small[:, SM_SSQ + sub:SM_SSQ + sub + 1],
                                                            func=AF.Ln, scale=1.0 / D, bias=eps_col[:, 0:1]), reads=[R_st[sub], R_eps], writes=[R_st[sub]])
                P.op("act", lambda e, sub=sub: e.activation(out=small[:, SM_RS + sub:SM_RS + sub + 1], in_=small[:, SM_LN + sub:SM_LN + sub + 1],
                                                            func=AF.Exp, scale=-0.5), reads=[R_st[sub]], writes=[R_st[sub]])
                P.op("dve", lambda e, sub=sub: e.scalar_tensor_tensor(out=x_tm[:, sub, :], in0=x_tm[:, sub, :], scalar=small[:, SM_RS + sub:SM_RS + sub + 1],
                                                                      in1=gfin[:], op0=ALU.mult, op1=ALU.mult),
                     reads=[R_x[sub], R_st[sub], R_const], writes=[R_x[sub]])
            if kind == "s":
                dst = y_s[:, :]
            else:
                r0 = tidx * TP + sub * 128
                dst = y_p[r0:r0 + 128, :]
            P.dma("act", lambda e, d=dst, sub=sub: e.dma_start(out=d, in_=x_tm[:, sub, :]), S_y[sub], reads=[R_x[sub]])

    eps_col = sb("eps_col", [128, 1], F32)
    R_eps = Res("eps")
    P.op("dve", lambda e: e.memset(eps_col[:], EPS), writes=[R_eps])

    def emit_ulast(T):
        for pr in range(4):
            cvw, cvr = wget("win", 4 + pr, 0)
            vvw, vvr = wget("win", 8 + pr, 0)
            for c in range(2):
                cc = 2 * pr + c
                bC, bCr = pf_next()
                bV, bVr = pf_next()
                for (b, br, vw, vr) in ((bC, bCr, cvw, cvr), (bV, bVr, vvw, vvr)):
                    for kc in range(KC):
                        P.op("pe", lambda e, b=b, vw=vw, kc=kc, c=c: e.matmul(b[:, 0:2], lhsT=vw[:, kc, c * 128:(c + 1) * 128], rhs=hT[:, kc, T - 2:T],
                                                                              start=(kc == 0), stop=(kc == KC - 1)), reads=[vr, R_hT], writes=[br])
                cs, csr = f32w[2], R_f32w[2]
                P.op("act", lambda e, bC=bC, cs=cs: e.copy(out=cs[:, 0:2], in_=bC[:, 0:2]), reads=[bCr], writes=[csr])
                P.op("dve", lambda e, bV=bV, cs=cs, cc=cc: e.tensor_tensor(out=xsmall[:, 16 + cc * 2:18 + cc * 2], in0=bV[:, 0:2], in1=cs[:, 0:2], op=ALU.mult),
                     reads=[bVr, csr], writes=[R_xsmall])

    def emit_tail(T, kind, tidx, segs):
        emit_norm(T, V_GM)
        emit_conv(T, segs, kind)
        emit_hgrn(T, kind)
        emit_merge(T)
        if kind == "s":
            for sq in range(2):
                P.dma("pool", lambda e, sq=sq: e.dma_start(out=conv_s_o[sq], in_=halo[:, :, sq, :]), S_cso, reads=[R_halo])
        if debug_stop == "mix":
            emit_final(T, kind, tidx, do_norm=False)
            return
        emit_norm(T, V_G2)
        emit_ffn(T, "g2", "u2", "d2")
        emit_final(T, kind, tidx)

    P.op("dve", lambda e: e.memset(xsmall[:, 0:32], 0.0), writes=[R_xsmall])
    if NPT > 0:
        emit_norm(TP, V_G1)
        emit_ffn(TP, "g1", "u1", "d1")
        emit_norm(TP, V_GM)
        emit_hgrn(TP, "p", pre=True)
        emit_ulast(TP)
    P.op("dve", lambda e: e.tensor_scalar(out=Sm[:].rearrange("p h v -> p (h v)"), in0=Sm[:].rearrange("p h v -> p (h v)"), scalar1=sel[:, 0:1],
                                          scalar2=None, op0=ALU.mult), reads=R_Sm + [R_const], writes=R_Sm)
    P.op("dve", lambda e: e.tensor_scalar(out=halo[:, :, 2, :], in0=xsmall[:, 16:32].rearrange("p (c t) -> p c t", t=2), scalar1=sel[:, 0:1],
                                          scalar2=None, op0=ALU.mult), reads=[R_xsmall, R_const, R_halo], writes=[R_halo])
    for (kind, tidx, T) in tiles[1:]:
        load_x(kind, tidx, T // 128)
        emit_norm(T, V_G1)
        emit_ffn(T, "g1", "u1", "d1")
        if debug_stop == "ffn1":
            emit_final(T, kind, tidx, do_norm=False)
        else:
            emit_tail(T, kind, tidx, [(0, T, 2, 0)])
    kind, tidx, T = tiles[0]
    load_x("s", 0, 1)
    emit_norm(T, V_G1)
    emit_ffn(T, "g1", "u1", "d1")
    if debug_stop == "ffn1":
        emit_final(T, kind, tidx, do_norm=False)
    else:
        emit_tail(T, kind, tidx, [(0, 64, 0, 0), (64, 128, 1, 66)])

    P.dma("pool", lambda e: e.dma_start(out=conv_p_o[:, :, :], in_=halo[:, :, 2, :]), S_misc_out, reads=[R_halo])
    P.dma("pool", lambda e: e.dma_start(out=st_p_o.rearrange("h k v -> k h v"), in_=Sm[:]), S_misc_out, reads=R_Sm)

    P.emit(nc, es)
    es.close()
    return nc, P


def _consts():
    ident = np.eye(128, dtype=np.float32).astype(ml_dtypes.bfloat16)
    ones = np.ones((128, 128), dtype=np.float32).astype(ml_dtypes.bfloat16)
    s = np.arange(64)[:, None]
    l = np.arange(64)[None, :]
    tri1 = (s <= l).astype(np.float32)
    tri = np.tile(tri1, (1, 8))
    rmask = np.ones((128, 512), dtype=np.float32)
    rmask[:, ::64] = 0.0
    return ident, ones, tri, rmask


def make_in_maps(inp, n_ptiles=N_PTILES):
    ident, ones, tri, rmask = _consts()
    fm = lambda v, n: np.ascontiguousarray(np.asarray(v, np.float32).reshape(n, 128).T)
    vecs = np.zeros((128, 128), np.float32)
    vecs[:, 0:16] = fm(inp["norm_ffn1"][0], 16)
    vecs[:, 16:32] = fm(inp["norm_mix"][0], 16)
    vecs[:, 32:48] = fm(inp["norm_ffn2"][0], 16)
    vecs[:, 48:64] = fm(inp["hgrn_norm"][0], 16)
    vecs[:, 64:80] = fm(inp["hgrn_lb_logits"][0], 16)
    vecs[:, 80:96] = fm(inp["hgrn_lb_logits"][1], 16)
    cw = np.asarray(inp["conv_w"][0], np.float32)
    vecs[:, 96:120] = cw.reshape(3, 8, 128).transpose(2, 1, 0).reshape(128, 24)
    gfin = np.ascontiguousarray(np.asarray(inp["norm_final"], np.float32).reshape(1, D))
    wnames = ["w_ffn1_gate", "w_ffn1_up", "w_ffn1_down", "w_ffn2_gate", "w_ffn2_up", "w_ffn2_down", "w_in", "w_br_conv",
              "w_br_hgrn", "w_out"]
    wd = {n: np.ascontiguousarray(np.asarray(inp[n], np.float32)[0]) for n in wnames}
    xp = np.asarray(inp["x_prompt"], np.float32)
    xs = np.asarray(inp["x_sample"], np.float32)
    cc = np.asarray(inp["cache_conv"], np.float32)[0]
    sh = np.asarray(inp["state_hgrn"], np.float32)[0]
    maps = []
    npt = PCH // TP
    for c in range(NCORES):
        b, q = c // 4, c % 4
        m = dict(wd)
        m["x_s"] = np.ascontiguousarray(xs[2 * c:2 * c + 2].reshape(128, D))
        m["x_p"] = np.ascontiguousarray(xp[b, q * PCH:q * PCH + npt * TP])
        m["halo_s"] = np.ascontiguousarray(cc[2 * c:2 * c + 2].reshape(2, 2, 8, 128).transpose(0, 3, 2, 1))
        m["st_s"] = np.ascontiguousarray(sh[2 * c:2 * c + 2])
        selv = np.zeros((128, 24), np.float32)
        selv[:, 0] = 1.0 if q > 0 else 0.0
        m["sel"] = selv
        m["x_w"] = np.ascontiguousarray(xp[b, q * PCH - TP:q * PCH]) if q > 0 else np.zeros((TP, D), np.float32)
        m["vecs"] = vecs
        m["gfin"] = gfin
        m["ident"] = ident
        m["ones"] = ones
        m["tri"] = tri
        m["rmask"] = rmask
        maps.append(m)
    return maps


def assemble(results, n_ptiles=N_PTILES):
    B, SEQ = 2, 16384
    y_prompt = np.zeros((B, SEQ, D), np.float32)
    y_sample = np.zeros((16, 64, D), np.float32)
    ncp = np.zeros((1, B, 2, DCONV), np.float32)
    nhp = np.zeros((1, B, NH, 128, 128), np.float32)
    ncs = np.zeros((1, 16, 2, DCONV), np.float32)
    nhs = np.zeros((1, 16, NH, 128, 128), np.float32)
    npt = PCH // TP
    for c in range(NCORES):
        r = results[c]
        b, q = c // 4, c % 4
        y_prompt[b, q * PCH:q * PCH + npt * TP] = r["y_p"]
        y_sample[2 * c:2 * c + 2] = r["y_s"].reshape(2, 64, D)
        cs = r["conv_s_o"]
        ncs[0, 2 * c:2 * c + 2] = cs.transpose(0, 3, 2, 1).reshape(2, 2, DCONV)
        nhs[0, 2 * c:2 * c + 2] = r["st_s_o"]
        if q == 3:
            ncp[0, b] = r["conv_p_o"].transpose(2, 1, 0).reshape(2, DCONV)
            nhp[0, b] = r["st_p_o"]
    return (y_prompt, y_sample, ncp, nhp, ncs, nhs)


_CACHE = {}


def kernel(**inputs):
    if "nc" not in _CACHE:
        _CACHE["nc"] = build_program()[0]
    nc = _CACHE["nc"]
    in_maps = make_in_maps(inputs)
    res = run_bass_kernel_spmd(nc, in_maps, core_ids=list(range(NCORES)))
    return assemble(res.results)
```

```python
import numpy as np
import ml_dtypes
from contextlib import ExitStack
import concourse.bass as bass
import concourse.mybir as mybir
from concourse.bass_utils import run_bass_kernel_spmd

F32 = mybir.dt.float32
BF16 = mybir.dt.bfloat16
AF = mybir.ActivationFunctionType
ALU = mybir.AluOpType

D = 2048
DFF = 5632
DCONV = 1024
NH = 16
DIN = 15360
NCORES = 8
PCH = 4096
TP = 512
EPS = 1e-6
LB = 64
KC = D // 128
NFF = DFF // 128

N_PTILES = PCH // TP
DEBUG_STOP = None
SAME_ENGINE_SYNC = True


class Res:
    __slots__ = ("w", "r", "name", "dead")

    def __init__(self, name=""):
        self.w = None
        self.r = {}
        self.name = name
        self.dead = False


class DmaSem:
    def __init__(self, key):
        self.key = key
        self.count = 0


class Prog:
    ENG = ["pe", "act", "dve", "pool", "sp"]

    def __init__(self):
        self.stream = {e: [] for e in self.ENG}
        self.seen = {e: {} for e in self.ENG}
        self.dsems = []

    def dsem(self, name):
        s = DmaSem(len(self.dsems))
        s.name = name
        self.dsems.append(s)
        return s

    def _need(self, e, tok, waits):
        if tok is None:
            return
        if tok[0] == "E":
            f, idx = tok[1], tok[2]
            if f == e and (e == "pe" or not SAME_ENGINE_SYNC):
                return
            key = ("E", f)
        else:
            key = ("D", tok[1])
            idx = tok[2]
        if self.seen[e].get(key, -1) >= idx:
            return
        self.seen[e][key] = idx
        waits.append(tok)

    def _deps(self, e, reads, writes):
        waits = []
        for r in list(reads) + list(writes):
            assert not r.dead, "stale PSUM bank generation used: " + r.name
        for r in reads:
            self._need(e, r.w, waits)
        for w in writes:
            self._need(e, w.w, waits)
            for t in w.r.values():
                self._need(e, t, waits)
        return waits

    def op(self, e, fn, reads=(), writes=()):
        waits = self._deps(e, reads, writes)
        idx = len(self.stream[e])
        self.stream[e].append([waits, fn, False, None, 0])
        tok = ("E", e, idx)
        for r in reads:
            r.r[e] = tok
        for w in writes:
            w.w = tok
            w.r = {}
        return tok

    def dma(self, q, fn, sem, reads=(), writes=(), inc=16):
        waits = self._deps(q, reads, writes)
        sem.count += inc
        tok = ("D", sem.key, sem.count)
        self.stream[q].append([waits, fn, False, sem, inc])
        for r in reads:
            r.r[("D", sem.key)] = tok
        for w in writes:
            w.w = tok
            w.r = {}
        return tok

    def emit(self, nc, es):
        for e in self.ENG:
            for ent in self.stream[e]:
                for tok in ent[0]:
                    if tok[0] == "E":
                        self.stream[tok[1]][tok[2]][2] = True
        val = {}
        for e in self.ENG:
            c = 0
            v = []
            for ent in self.stream[e]:
                if ent[2]:
                    c += 1
                v.append(c)
            val[e] = v
        self.nflag = {e: (val[e][-1] if val[e] else 0) for e in self.ENG}
        esem = {e: es.enter_context(nc.semaphore("es_" + e)) for e in self.ENG}
        dsem = [es.enter_context(nc.semaphore("ds_%d" % s.key)) for s in self.dsems]
        blk = es.enter_context(nc.Block())
        stream = self.stream

        def replay(name, eng, final=False):
            for waits, fn, flag, dm, dinc in stream[name]:
                for tok in waits:
                    if tok[0] == "E":
                        eng.wait_ge(esem[tok[1]], val[tok[1]][tok[2]])
                    else:
                        eng.wait_ge(dsem[tok[1]], tok[2])
                inst = fn(eng)
                if dm is not None:
                    if dinc == 16:
                        inst.then_inc(dsem[dm.key], 16)
                    else:
                        inst.then_inc(dsem[dm.key])
                elif flag:
                    inst.then_inc(esem[name], 1)
            if final:
                for s in self.dsems:
                    if s.count:
                        eng.wait_ge(dsem[s.key], s.count)

        @blk.tensor
        def _(e):
            replay("pe", e)

        @blk.scalar
        def _(e):
            replay("act", e)

        @blk.vector
        def _(e):
            replay("dve", e)

        @blk.gpsimd
        def _(e):
            replay("pool", e, final=True)

        @blk.sync
        def _(e):
            replay("sp", e)


class WMat:
    def __init__(self, nc, name, src, K, N, kcs, cols):
        self.name, self.src, self.K, self.N, self.kcs, self.cols = name, src, K, N, kcs, cols
        self.ncg = N // cols
        self.nkc = K // 128
        self.nks = (self.nkc + kcs - 1) // kcs
        self.sc = nc.dram_tensor("sc_" + name, [self.ncg * self.nks, 128, kcs * cols], BF16, kind="Internal").ap()
        self.res = Res("sc_" + name)

    def kc_count(self, ks):
        return min(self.kcs, self.nkc - ks * self.kcs)

    def slab_id(self, cg, ks):
        return cg * self.nks + ks


def build_program(n_ptiles=N_PTILES, debug_stop=DEBUG_STOP):
    nc = bass.Bass("TRN2", target_bir_lowering=False)
    P = Prog()
    es = ExitStack()
    NPT = n_ptiles
    npt_alloc = PCH // TP

    def din(name, shape, dt=F32):
        return nc.dram_tensor(name, shape, dt, kind="ExternalInput").ap()

    def dout(name, shape, dt=F32):
        return nc.dram_tensor(name, shape, dt, kind="ExternalOutput").ap()

    x_s = din("x_s", [128, D])
    x_p = din("x_p", [npt_alloc * TP, D])
    halo_s = din("halo_s", [2, 128, 8, 2])
    st_s = din("st_s", [2, NH, 128, 128])
    vecs_d = din("vecs", [128, 128])
    gfin_d = din("gfin", [1, D])
    ident_d = din("ident", [128, 128], BF16)
    ones_d = din("ones", [128, 128], BF16)
    tri_d = din("tri", [64, 512])
    rmask_d = din("rmask", [128, 512])
    w_in_d = {
        "g1": din("w_ffn1_gate", [D, DFF]), "u1": din("w_ffn1_up", [D, DFF]), "d1": din("w_ffn1_down", [DFF, D]),
        "g2": din("w_ffn2_gate", [D, DFF]), "u2": din("w_ffn2_up", [D, DFF]), "d2": din("w_ffn2_down", [DFF, D]),
        "win": din("w_in", [D, DIN]), "bc": din("w_br_conv", [DCONV, D]), "bh": din("w_br_hgrn", [D, D]),
        "wo": din("w_out", [D, D]),
    }
    sel_d = din("sel", [128, 24])
    x_w = din("x_w", [TP, D])
    y_s = dout("y_s", [128, D])
    y_p = dout("y_p", [npt_alloc * TP, D])
    conv_s_o = dout("conv_s_o", [2, 128, 8, 2])
    conv_p_o = dout("conv_p_o", [128, 8, 2])
    st_s_o = dout("st_s_o", [2, NH, 128, 128])
    st_p_o = dout("st_p_o", [NH, 128, 128])

    W = {
        "g1": WMat(nc, "g1", w_in_d["g1"], D, DFF, 16, 256), "u1": WMat(nc, "u1", w_in_d["u1"], D, DFF, 16, 256),
        "d1": WMat(nc, "d1", w_in_d["d1"], DFF, D, 8, 512),
        "g2": WMat(nc, "g2", w_in_d["g2"], D, DFF, 16, 256), "u2": WMat(nc, "u2", w_in_d["u2"], D, DFF, 16, 256),
        "d2": WMat(nc, "d2", w_in_d["d2"], DFF, D, 8, 512),
        "win": WMat(nc, "win", w_in_d["win"], D, DIN, 16, 256),
        "bc": WMat(nc, "bc", w_in_d["bc"], DCONV, D, 8, 256), "bh": WMat(nc, "bh", w_in_d["bh"], D, D, 16, 256),
        "wo": WMat(nc, "wo", w_in_d["wo"], D, D, 8, 512),
    }

    def sb(name, shape, dt):
        return es.enter_context(nc.sbuf_tensor("sb_" + name, shape, dt))

    x_tm = sb("x_tm", [128, 4, D], F32)
    hn_tm = sb("hn_tm", [128, 2, D], BF16)
    hT = sb("hT", [128, KC, TP], BF16)
    big = sb("big", [128, NFF, TP], BF16)
    NSLOT = 5
    wring = sb("wring", [128, NSLOT, 4096], BF16)
    gfin = sb("gfin", [128, D], F32)
    vecs = sb("vecs", [128, 128], F32)
    ident = sb("ident", [128, 128], BF16)
    ones = sb("ones", [128, 128], BF16)
    tri = sb("tri", [64, 512], F32)
    rmask = sb("rmask", [128, 512], F32)
    small = sb("small", [128, 64], F32)
    Sm = sb("Sm", [128, NH, 128], F32)
    Stmp = sb("Stmp", [128, 2, 2, 128], F32)
    halo = sb("halo", [128, 8, 3, 2], F32)
    f32w = [sb("f32w%d" % i, [128, 516], F32) for i in range(12)]
    bfw = [sb("bfw%d" % i, [128, 512], BF16) for i in range(10)]
    ke_tm = sb("ke_tm", [64, 8, 128], BF16)
    v_tm = sb("v_tm", [64, 8, 128], BF16)
    AT_bf = sb("AT_bf", [64, 512], BF16)
    Sbf = sb("Sbf", [128, 8, 128], BF16)
    sel = sb("sel", [128, 24], F32)
    xsmall = sb("xsmall", [128, 32], F32)

    NPF = 8
    pf = [es.enter_context(nc.psum_tensor("pf%d" % i, [128, 512], F32)) for i in range(NPF)]
    pf_res = [Res("pf%d" % i) for i in range(NPF)]
    pf_ctr = [0]

    def _regen(lst, i):
        old = lst[i]
        new = Res(old.name)
        new.w, new.r = old.w, old.r
        old.dead = True
        lst[i] = new
        return new

    def pf_next():
        i = pf_ctr[0] % NPF
        pf_ctr[0] += 1
        return pf[i], _regen(pf_res, i)

    def pb_next():
        i = pf_ctr[0] % NPF
        pf_ctr[0] += 1
        return pf[i][:].bitcast(BF16), _regen(pf_res, i)

    R_x = [Res("x%d" % i) for i in range(4)]
    R_hn = [Res("hn%d" % i) for i in range(2)]
    R_hT = Res("hT")
    R_big = [Res("big%d" % i) for i in range(NFF)]
    R_slot = [Res("slot%d" % i) for i in range(NSLOT)]
    R_const = Res("const")
    R_small = Res("small")
    R_st = [Res("st%d" % i) for i in range(4)]
    R_Sm = [Res("Sm%d" % i) for i in range(NH)]
    R_Stmp = [[Res("Stmp%d_%d" % (i, j)) for j in range(2)] for i in range(2)]
    R_halo = Res("halo")
    R_f32w = [Res("f32w%d" % i) for i in range(12)]
    R_bfw = [Res("bfw%d" % i) for i in range(10)]
    R_ke_tm, R_v_tm, R_AT, R_Sbf = Res("ke_tm"), Res("v_tm"), Res("AT"), Res("Sbf")
    R_xsmall = Res("xsmall")

    S_slot = [P.dsem("slot%d" % i) for i in range(NSLOT)]
    S_const = P.dsem("const")
    S_x = [P.dsem("x%d" % i) for i in range(4)]
    S_y = [P.dsem("y%d" % i) for i in range(4)]
    S_misc_in = P.dsem("misc_in")
    S_misc_out = P.dsem("misc_out")
    S_sti = [P.dsem("sti%d" % i) for i in range(2)]
    S_sto = [P.dsem("sto%d" % i) for i in range(2)]
    S_cso = P.dsem("cso")

    def ld_const(dst, src):
        P.dma("pool", lambda e, d=dst, s=src: e.dma_start(out=d, in_=s), S_const, writes=[R_const])

    ld_const(vecs[:], vecs_d[:, :])
    ld_const(ident[:], ident_d[:, :])
    ld_const(ones[:], ones_d[:, :])
    ld_const(tri[:], tri_d[:, :])
    ld_const(rmask[:], rmask_d[:, :])
    ld_const(sel[:], sel_d[:, :])
    ld_const(gfin[:], gfin_d.broadcast_to([128, D]))
    R_const.w = ("D", S_const.key, S_const.count)

    def load_x(tile_kind, tile_idx, nsub):
        for sub in range(nsub):
            if tile_kind == "s":
                src = x_s[:, :]
            else:
                r0 = tile_idx * TP + sub * 128
                src = x_p[r0:r0 + 128, :]
            P.dma("act", lambda e, d=x_tm[:, sub, :], s=src: e.dma_start(out=d, in_=s), S_x[sub], writes=[R_x[sub]])

    tiles = [("s", 0, 128)] + [("p", i, TP) for i in range(NPT)]
    if NPT > 0:
        for sub in range(TP // 128):
            P.dma("act", lambda e, sub=sub: e.dma_start(out=x_tm[:, sub, :], in_=x_w[sub * 128:(sub + 1) * 128, :]), S_x[sub], writes=[R_x[sub]])

    for sq in range(2):
        P.dma("pool", lambda e, d=halo[:, :, sq, :], s=halo_s[sq]: e.dma_start(out=d, in_=s), S_misc_in, writes=[R_halo])
    R_halo.w = ("D", S_misc_in.key, S_misc_in.count)

    def cast_group(key, sids, sem):
        wm = W[key]
        for sid in sids:
            cg, ks = sid // wm.nks, sid % wm.nks
            nk = wm.kc_count(ks)
            for k0 in range(0, nk, 4):
                k1 = min(nk, k0 + 4)
                r0 = (ks * wm.kcs + k0) * 128
                r1 = (ks * wm.kcs + k1) * 128
                src = wm.src[r0:r1, cg * wm.cols:(cg + 1) * wm.cols].rearrange("(kc p) c -> p kc c", p=128)
                dst = wm.sc[sid].rearrange("p (kc c) -> p kc c", c=wm.cols)[:, k0:k1, :]
                P.dma("pool", lambda e, d=dst, s=src: e.dma_start(out=d, in_=s), sem)
        r = Res("cast")
        r.w = ("D", sem.key, sem.count)
        for sid in sids:
            wm.res_of[sid] = r

    for wm in W.values():
        wm.res_of = {}
    qs4 = [range(0, 6), range(6, 12), range(12, 17), range(17, 22)]
    for gi in range(4):
        cast_group("g1", qs4[gi], P.dsem("c_g1_%d" % gi))
        cast_group("u1", qs4[gi], P.dsem("c_u1_%d" % gi))
    for cg in range(4):
        cast_group("d1", range(cg * 6, cg * 6 + 6), P.dsem("c_d1_%d" % cg))
    for gi, sids in enumerate([range(20, 28), range(28, 36), range(4, 12), list(range(0, 4)) + list(range(12, 20)), range(36, 44), range(44, 60)]):
        cast_group("win", sids, P.dsem("c_win_%d" % gi))
    for key in ["bc", "bh", "wo", "g2", "u2", "d2"]:
        wmk = W[key]
        cast_group(key, range(wmk.ncg * wmk.nks), P.dsem("c_" + key))

    ring_ctr = [0]

    def wget(key, cg, ks):
        wm = W[key]
        i = ring_ctr[0] % NSLOT
        ring_ctr[0] += 1
        nk = wm.kc_count(ks)
        n = nk * wm.cols
        sid = wm.slab_id(cg, ks)
        dst = wring[:, i, 0:n]
        src = wm.sc[sid][:, 0:n]
        P.dma("sp", lambda e, d=dst, s=src: e.dma_start(out=d, in_=s), S_slot[i], reads=[wm.res_of[sid]], writes=[R_slot[i]])
        view = wring[:, i, 0:n].rearrange("p (kc c) -> p kc c", c=wm.cols)
        return view, R_slot[i]

    vcol = lambda c0, n=1: vecs[:, c0:c0 + n]
    V_G1, V_GM, V_G2, V_HN, V_L0, V_L1, V_CW = 0, 16, 32, 48, 64, 80, 96
    SM_SSQ, SM_LN, SM_RS, SM_LB, SM_OMLB = 0, 4, 8, 16, 32

    P.op("dve", lambda e: e.tensor_tensor(out=small[:, SM_LB:SM_LB + 16], in0=vcol(V_L0, 16), in1=vcol(V_L1, 16), op=ALU.subtract),
         reads=[R_const], writes=[R_small])
    P.op("act", lambda e: e.activation(out=small[:, SM_LB:SM_LB + 16], in_=small[:, SM_LB:SM_LB + 16], func=AF.Sigmoid),
         reads=[R_small], writes=[R_small])
    P.op("dve", lambda e: e.tensor_scalar(out=small[:, SM_OMLB:SM_OMLB + 16], in0=small[:, SM_LB:SM_LB + 16], scalar1=-1.0, scalar2=1.0,
                                          op0=ALU.mult, op1=ALU.add), reads=[R_small], writes=[R_small])
    R_lb = Res("lb")
    R_lb.w = R_small.w
    P.op("dve", lambda e: e.memset(Sm[:].rearrange("p h v -> p (h v)"), 0.0), writes=R_Sm)

    def emit_norm(T, gcol0):
        nsub = T // 128

        def stats(sub):
            hs = sub % 2
            P.op("act", lambda e: e.activation(out=hn_tm[:, hs, :], in_=x_tm[:, sub, :], func=AF.Square,
                                               accum_out=small[:, SM_SSQ + sub:SM_SSQ + sub + 1]),
                 reads=[R_x[sub]], writes=[R_hn[hs], R_st[sub]])
            P.op("act", lambda e: e.activation(out=small[:, SM_LN + sub:SM_LN + sub + 1], in_=small[:, SM_SSQ + sub:SM_SSQ + sub + 1],
                                               func=AF.Ln, scale=1.0 / D, bias=eps_col[:, 0:1]), reads=[R_st[sub], R_eps], writes=[R_st[sub]])
            P.op("act", lambda e: e.activation(out=small[:, SM_RS + sub:SM_RS + sub + 1], in_=small[:, SM_LN + sub:SM_LN + sub + 1],
                                               func=AF.Exp, scale=-0.5), reads=[R_st[sub]], writes=[R_st[sub]])
            P.op("dve", lambda e: e.tensor_scalar(out=hn_tm[:, hs, :], in0=x_tm[:, sub, :], scalar1=small[:, SM_RS + sub:SM_RS + sub + 1],
                                                  scalar2=None, op0=ALU.mult), reads=[R_x[sub], R_st[sub]], writes=[R_hn[hs]])

        def xpose(sub):
            hs = sub % 2
            for half in range(2):
                bank, bres = pb_next()
                for k8 in range(8):
                    kc = half * 8 + k8
                    P.op("pe", lambda e, k8=k8, kc=kc, bank=bank: e.transpose(out=bank[:, k8 * 128:(k8 + 1) * 128], in_=hn_tm[:, hs, kc * 128:(kc + 1) * 128],
                                                                              identity=ident[:]), reads=[R_hn[hs], R_const], writes=[bres])
                for k8 in range(8):
                    kc = half * 8 + k8
                    if k8 in (0, 3, 6):
                        P.op("act", lambda e, k8=k8, kc=kc, bank=bank: e.activation(out=hT[:, kc, sub * 128:(sub + 1) * 128], in_=bank[:, k8 * 128:(k8 + 1) * 128],
                                                                         func=AF.Copy, scale=vecs[:, gcol0 + kc:gcol0 + kc + 1]),
                             reads=[bres, R_const], writes=[R_hT])
                    else:
                        P.op("dve", lambda e, k8=k8, kc=kc, bank=bank: e.tensor_scalar(out=hT[:, kc, sub * 128:(sub + 1) * 128], in0=bank[:, k8 * 128:(k8 + 1) * 128],
                                                                            scalar1=vecs[:, gcol0 + kc:gcol0 + kc + 1], scalar2=None, op0=ALU.mult),
                             reads=[bres, R_const], writes=[R_hT])

        stats(0)
        for sub in range(nsub):
            if sub + 1 < nsub:
                stats(sub + 1)
            xpose(sub)

    def emit_ffn(T, kg, ku, kd):
        nsub = T // 128
        for s in range(DFF // 256):
            gv, gr = wget(kg, s, 0)
            uv, ur = wget(ku, s, 0)
            for c in range(2):
                j = 2 * s + c
                bg, bgr = pf_next()
                bu, bur = pf_next()
                for kc in range(KC):
                    P.op("pe", lambda e, bg=bg, gv=gv, kc=kc, c=c: e.matmul(bg[:, 0:T], lhsT=gv[:, kc, c * 128:(c + 1) * 128], rhs=hT[:, kc, 0:T],
                                                                            start=(kc == 0), stop=(kc == KC - 1)),
                         reads=[gr, R_hT], writes=[bgr])
                for kc in range(KC):
                    P.op("pe", lambda e, bu=bu, uv=uv, kc=kc, c=c: e.matmul(bu[:, 0:T], lhsT=uv[:, kc, c * 128:(c + 1) * 128], rhs=hT[:, kc, 0:T],
                                                                            start=(kc == 0), stop=(kc == KC - 1)),
                         reads=[ur, R_hT], writes=[bur])
                wi = j % 2
                P.op("act", lambda e, bg=bg, wi=wi: e.activation(out=f32w[wi][:, 0:T], in_=bg[:, 0:T], func=AF.Silu),
                     reads=[bgr], writes=[R_f32w[wi]])
                P.op("dve", lambda e, bu=bu, wi=wi, j=j: e.tensor_tensor(out=big[:, j, 0:T], in0=bu[:, 0:T], in1=f32w[wi][:, 0:T], op=ALU.mult),
                     reads=[bur, R_f32w[wi]], writes=[R_big[j]])
        wm = W[kd]
        for cg in range(4):
            banks = [pf_next() for _ in range(nsub)]
            for ks in range(wm.nks):
                dv, dr = wget(kd, cg, ks)
                for kk in range(wm.kc_count(ks)):
                    k = ks * wm.kcs + kk
                    for sub in range(nsub):
                        b, br = banks[sub]
                        P.op("pe", lambda e, b=b, dv=dv, kk=kk, k=k, sub=sub: e.matmul(b[:, :], lhsT=big[:, k, sub * 128:(sub + 1) * 128], rhs=dv[:, kk, :],
                                                                                       start=(k == 0), stop=(k == NFF - 1)),
                             reads=[dr, R_big[k]], writes=[br])
            for sub in range(nsub):
                b, br = banks[sub]
                P.op("dve", lambda e, b=b, sub=sub, cg=cg: e.scalar_tensor_tensor(out=x_tm[:, sub, cg * 512:(cg + 1) * 512], in0=b[:, :], scalar=0.5,
                                                                                  in1=x_tm[:, sub, cg * 512:(cg + 1) * 512], op0=ALU.mult, op1=ALU.add),
                     reads=[br, R_x[sub]], writes=[R_x[sub]])

    def proj_fm(T, view, vres, c):
        b, br = pf_next()
        for kc in range(KC):
            P.op("pe", lambda e, b=b, kc=kc: e.matmul(b[:, 0:T], lhsT=view[:, kc, c * 128:(c + 1) * 128], rhs=hT[:, kc, 0:T],
                                                      start=(kc == 0), stop=(kc == KC - 1)), reads=[vres, R_hT], writes=[br])
        return b, br

    ycT = lambda cc, T: big[:, cc, 0:T]
    yhT = lambda h, T: big[:, 8 + h, 0:T]
    mT = lambda oc, T: big[:, 24 + oc, 0:T]
    R_yc = R_big[0:8]
    R_yh = R_big[8:24]
    R_m = R_big[24:40]

    def emit_conv(T, segs, kind):
        for pr in range(4):
            cvw, cvr = wget("win", 4 + pr, 0)
            vvw, vvr = wget("win", 8 + pr, 0)
            bvw, bvr = wget("win", 0 + pr, 0)
            for c in range(2):
                cc = 2 * pr + c
                bC, bCr = proj_fm(T, cvw, cvr, c)
                bV, bVr = proj_fm(T, vvw, vvr, c)
                bB, bBr = proj_fm(T, bvw, bvr, c)
                cs, csr = f32w[2], R_f32w[2]
                ub, ubr = f32w[3], R_f32w[3]
                yv, yvr = f32w[4], R_f32w[4]
                P.op("act", lambda e, bC=bC, cs=cs: e.copy(out=cs[:, 0:T], in_=bC[:, 0:T]), reads=[bCr], writes=[csr])
                for (t0, t1, sg, o) in segs:
                    n = t1 - t0
                    P.op("act", lambda e, ub=ub, o=o, cc=cc, sg=sg: e.copy(out=ub[:, o:o + 2], in_=halo[:, cc, sg, :]),
                         reads=[R_halo], writes=[ubr])
                    P.op("dve", lambda e, ub=ub, o=o, n=n, t0=t0, t1=t1, bV=bV, cs=cs: e.tensor_tensor(out=ub[:, o + 2:o + 2 + n], in0=bV[:, t0:t1],
                                                                                                      in1=cs[:, t0:t1], op=ALU.mult),
                         reads=[bVr, csr], writes=[ubr])
                    P.op("act", lambda e, ub=ub, o=o, n=n, cc=cc, sg=sg: e.copy(out=halo[:, cc, sg, :], in_=ub[:, o + n:o + n + 2]),
                         reads=[ubr], writes=[R_halo])
                    w0 = vecs[:, V_CW + cc * 3 + 0:V_CW + cc * 3 + 1]
                    w1 = vecs[:, V_CW + cc * 3 + 1:V_CW + cc * 3 + 2]
                    w2 = vecs[:, V_CW + cc * 3 + 2:V_CW + cc * 3 + 3]
                    P.op("dve", lambda e, yv=yv, ub=ub, o=o, n=n, t0=t0, t1=t1, w0=w0: e.tensor_scalar(out=yv[:, t0:t1], in0=ub[:, o:o + n], scalar1=w0,
                                                                                                      scalar2=None, op0=ALU.mult),
                         reads=[ubr, R_const], writes=[yvr])
                    P.op("dve", lambda e, yv=yv, ub=ub, o=o, n=n, t0=t0, t1=t1, w1=w1: e.scalar_tensor_tensor(out=yv[:, t0:t1], in0=ub[:, o + 1:o + 1 + n],
                                                                                                             scalar=w1, in1=yv[:, t0:t1], op0=ALU.mult, op1=ALU.add),
                         reads=[ubr, R_const, yvr], writes=[yvr])
                    P.op("dve", lambda e, yv=yv, ub=ub, o=o, n=n, t0=t0, t1=t1, w2=w2: e.scalar_tensor_tensor(out=yv[:, t0:t1], in0=ub[:, o + 2:o + 2 + n],
                                                                                                             scalar=w2, in1=yv[:, t0:t1], op0=ALU.mult, op1=ALU.add),
                         reads=[ubr, R_const, yvr], writes=[yvr])
                P.op("dve", lambda e, yv=yv, bB=bB, cc=cc: e.tensor_tensor(out=ycT(cc, T), in0=bB[:, 0:T], in1=yv[:, 0:T], op=ALU.mult),
                     reads=[bBr, yvr], writes=[R_yc[cc]])

    def emit_hgrn(T, kind, pre=False):
        nb = T // LB
        slabs = {}
        qs, qsr = f32w[0], R_f32w[0]
        fg, fgr = f32w[1], R_f32w[1]
        lf, lfr = f32w[2], R_f32w[2]
        kk, kkr = f32w[3], R_f32w[3]
        G, Gr = f32w[4], R_f32w[4]
        enG, enGr = f32w[6], R_f32w[6]
        sqb, sqr = bfw[4], R_bfw[4]
        r1, r1r = f32w[10], R_f32w[10]
        r2, r2r = f32w[11], R_f32w[11]

        def mk(h):
            p = h % 2
            c = {"h": h, "c": h % 2}
            c["eG"], c["eGr"] = (f32w[5], R_f32w[5]) if p == 0 else (f32w[8], R_f32w[8])
            c["so"], c["sor"] = (f32w[7], R_f32w[7]) if p == 0 else (f32w[9], R_f32w[9])
            for i, nm in enumerate(["vT", "qd", "ki", "ke"]):
                c[nm], c[nm + "r"] = bfw[i + 5 * p], R_bfw[i + 5 * p]
            return c

        def projFI(c):
            h = c["h"]
            if h % 2 == 0:
                hp = h // 2
                slabs["f"] = wget("win", 20 + hp, 0)
                slabs["i"] = wget("win", 28 + hp, 0)
                if not pre:
                    slabs["q"] = wget("win", 12 + hp, 0)
                    slabs["o"] = wget("win", 36 + hp, 0)
            c["bf"] = proj_fm(T, slabs["f"][0], slabs["f"][1], c["c"])
            c["bi"] = proj_fm(T, slabs["i"][0], slabs["i"][1], c["c"])

        def evacFI(c):
            bf_, bfr = c["bf"]
            bi, bir = c["bi"]
            vT, vTr = c["vT"], c["vTr"]
            P.op("act", lambda e: e.activation(out=fg[:, 0:T], in_=bf_[:, 0:T], func=AF.Sigmoid), reads=[bfr], writes=[fgr])
            P.op("act", lambda e: e.copy(out=vT[:, 0:T], in_=bi[:, 0:T]), reads=[bir], writes=[vTr])

        def projQO(c):
            if pre:
                return
            c["bq"] = proj_fm(T, slabs["q"][0], slabs["q"][1], c["c"])
            c["bo"] = proj_fm(T, slabs["o"][0], slabs["o"][1], c["c"])

        def evacQO(c):
            if pre:
                return
            bq, bqr = c["bq"]
            bo, bor = c["bo"]
            so, sor = c["so"], c["sor"]
            P.op("act", lambda e: e.activation(out=qs[:, 0:T], in_=bq[:, 0:T], func=AF.Silu), reads=[bqr], writes=[qsr])
            P.op("act", lambda e: e.activation(out=so[:, 0:T], in_=bo[:, 0:T], func=AF.Silu), reads=[bor], writes=[sor])

        def E1(c):
            h = c["h"]
            eG, eGr, ki, kir, ke, ker = c["eG"], c["eGr"], c["ki"], c["kir"], c["ke"], c["ker"]
            lbc = small[:, SM_LB + h:SM_LB + h + 1]
            omc = small[:, SM_OMLB + h:SM_OMLB + h + 1]
            P.op("dve", lambda e: e.tensor_scalar(out=fg[:, 0:T], in0=fg[:, 0:T], scalar1=omc, scalar2=lbc, op0=ALU.mult, op1=ALU.add),
                 reads=[fgr, R_lb], writes=[fgr])
            P.op("act", lambda e: e.activation(out=lf[:, 0:T], in_=fg[:, 0:T], func=AF.Ln), reads=[fgr], writes=[lfr])
            P.op("dve", lambda e: e.tensor_scalar(out=kk[:, 0:T], in0=fg[:, 0:T], scalar1=-1.0, scalar2=1.0, op0=ALU.mult, op1=ALU.add),
                 reads=[fgr], writes=[kkr])
            P.op("dve", lambda e: e.tensor_tensor_scan(out=G[:, 0:T], data0=rmask[:, 0:T], data1=lf[:, 0:T], initial=0.0, op0=ALU.mult, op1=ALU.add),
                 reads=[lfr, R_const], writes=[Gr])
            P.op("act", lambda e: e.activation(out=eG[:, 0:T], in_=G[:, 0:T], func=AF.Exp), reads=[Gr], writes=[eGr])
            P.op("act", lambda e: e.activation(out=enG[:, 0:T], in_=G[:, 0:T], func=AF.Exp, scale=-1.0), reads=[Gr], writes=[enGr])
            P.op("dve", lambda e: e.tensor_tensor(out=ki[:, 0:T], in0=kk[:, 0:T], in1=enG[:, 0:T], op=ALU.mult), reads=[kkr, enGr], writes=[kir])
            for b in range(nb):
                P.op("dve", lambda e, b=b: e.tensor_scalar(out=ke[:, b * LB:(b + 1) * LB], in0=ki[:, b * LB:(b + 1) * LB],
                                                           scalar1=eG[:, b * LB + LB - 1:b * LB + LB], scalar2=None, op0=ALU.mult),
                     reads=[kir, eGr], writes=[ker])

        def E2(c):
            if pre:
                return
            eG, eGr, qd, qdr = c["eG"], c["eGr"], c["qd"], c["qdr"]
            P.op("dve", lambda e: e.tensor_tensor(out=qd[:, 0:T], in0=qs[:, 0:T], in1=eG[:, 0:T], op=ALU.mult), reads=[qsr, eGr], writes=[qdr])

        def trans(c):
            ke, ker, vT, vTr = c["ke"], c["ker"], c["vT"], c["vTr"]
            bT1, bT1r = pb_next()
            bT2, bT2r = pb_next()
            for b in range(nb):
                P.op("pe", lambda e, b=b: e.transpose(out=bT1[0:64, b * 128:(b + 1) * 128], in_=ke[:, b * LB:(b + 1) * LB], identity=ident[:]),
                     reads=[ker, R_const], writes=[bT1r])
            for b in range(nb):
                P.op("pe", lambda e, b=b: e.transpose(out=bT2[0:64, b * 128:(b + 1) * 128], in_=vT[:, b * LB:(b + 1) * LB], identity=ident[:]),
                     reads=[vTr, R_const], writes=[bT2r])
            P.op("act", lambda e: e.copy(out=ke_tm[:, 0:nb, :].rearrange("p b d -> p (b d)"), in_=bT1[0:64, 0:nb * 128]), reads=[bT1r], writes=[R_ke_tm])
            P.op("dve", lambda e: e.tensor_copy(out=v_tm[:, 0:nb, :].rearrange("p b d -> p (b d)"), in_=bT2[0:64, 0:nb * 128]), reads=[bT2r], writes=[R_v_tm])

        def AU(c):
            ki, kir, qd, qdr = c["ki"], c["kir"], c["qd"], c["qdr"]
            if not pre:
                bA, bAr = pf_next()
                for b in range(nb):
                    P.op("pe", lambda e, b=b: e.matmul(bA[0:64, b * LB:(b + 1) * LB], lhsT=ki[:, b * LB:(b + 1) * LB], rhs=qd[:, b * LB:(b + 1) * LB],
                                                       start=True, stop=True), reads=[kir, qdr], writes=[bAr])
                P.op("dve", lambda e: e.tensor_tensor(out=AT_bf[:, 0:T], in0=bA[0:64, 0:T], in1=tri[:, 0:T], op=ALU.mult), reads=[bAr, R_const], writes=[R_AT])
            ubanks = []
            for b in range(nb):
                if b % 4 == 0:
                    ubanks.append(pf_next())
                bU, bUr = ubanks[-1]
                P.op("pe", lambda e, bU=bU, b=b: e.matmul(bU[:, (b % 4) * 128:(b % 4 + 1) * 128], lhsT=ke_tm[:, b, :], rhs=v_tm[:, b, :], start=True, stop=True),
                     reads=[R_ke_tm, R_v_tm], writes=[bUr])
            c["ubanks"] = ubanks

        def chain(c):
            h, eG, eGr, ubanks = c["h"], c["eG"], c["eGr"], c["ubanks"]
            if kind == "s":
                slot = h % 2
                for b in range(nb):
                    P.dma("pool", lambda e, b=b: e.dma_start(out=Stmp[:, slot, b, :], in_=st_s[b, h]), S_sti[slot], writes=[R_Stmp[slot][b]])
                for b in range(nb):
                    R_Stmp[slot][b].w = ("D", S_sti[slot].key, S_sti[slot].count)
                S_of = lambda b: (Stmp[:, slot, b, :], R_Stmp[slot][b])
            else:
                S_of = lambda b: (Sm[:, h, :], R_Sm[h])
            for b in range(nb):
                Sap, Sr = S_of(b)
                bU, bUr = ubanks[b // 4]
                if not pre:
                    P.op("act", lambda e, Sap=Sap, b=b: e.copy(out=Sbf[:, b, :], in_=Sap), reads=[Sr], writes=[R_Sbf])
                P.op("dve", lambda e, Sap=Sap, bU=bU, b=b: e.scalar_tensor_tensor(out=Sap, in0=Sap, scalar=eG[:, b * LB + LB - 1:b * LB + LB],
                                                                                 in1=bU[:, (b % 4) * 128:(b % 4 + 1) * 128], op0=ALU.mult, op1=ALU.add),
                     reads=[Sr, eGr, bUr], writes=[Sr])
                if kind == "s":
                    P.dma("pool", lambda e, Sap=Sap, b=b: e.dma_start(out=st_s_o[b, h], in_=Sap), S_sto[slot], reads=[Sr])

        def oT(c):
            if pre:
                return
            qd, qdr = c["qd"], c["qdr"]
            bO, bOr = pf_next()
            for b in range(nb):
                P.op("pe", lambda e, b=b: e.matmul(bO[:, b * LB:(b + 1) * LB], lhsT=v_tm[:, b, :], rhs=AT_bf[:, b * LB:(b + 1) * LB], start=True, stop=False),
                     reads=[R_v_tm, R_AT], writes=[bOr])
                P.op("pe", lambda e, b=b: e.matmul(bO[:, b * LB:(b + 1) * LB], lhsT=Sbf[:, b, :], rhs=qd[:, b * LB:(b + 1) * LB], start=False, stop=True),
                     reads=[R_Sbf, qdr], writes=[bOr])
            c["bO"] = (bO, bOr)

        def F1(c):
            if pre:
                return
            bO, bOr = c["bO"]
            P.op("act", lambda e: e.activation(out=sqb[:, 0:T], in_=bO[:, 0:T], func=AF.Square), reads=[bOr], writes=[sqr])

        def F2a(c):
            if pre or c is None:
                return
            bN, bNr = pf_next()
            c["bN"] = (bN, bNr)
            P.op("pe", lambda e: e.matmul(bN[:, 0:T], lhsT=ones[:], rhs=sqb[:, 0:T], start=True, stop=True), reads=[sqr, R_const], writes=[bNr])

        def F2b(c):
            if pre or c is None:
                return
            h, so, sor = c["h"], c["so"], c["sor"]
            bO, bOr = c["bO"]
            bN, bNr = c["bN"]
            P.op("act", lambda e: e.activation(out=r1[:, 0:T], in_=bN[:, 0:T], func=AF.Ln, scale=1.0 / 128, bias=eps_col[:, 0:1]),
                 reads=[bNr, R_eps], writes=[r1r])
            P.op("act", lambda e: e.activation(out=r1[:, 0:T], in_=r1[:, 0:T], func=AF.Exp, scale=-0.5), reads=[r1r], writes=[r1r])
            hn_c = vecs[:, V_HN + h:V_HN + h + 1]
            P.op("dve", lambda e: e.scalar_tensor_tensor(out=r2[:, 0:T], in0=bO[:, 0:T], scalar=hn_c, in1=r1[:, 0:T], op0=ALU.mult, op1=ALU.mult),
                 reads=[bOr, r1r, R_const], writes=[r2r])
            P.op("dve", lambda e: e.tensor_tensor(out=yhT(h, T), in0=r2[:, 0:T], in1=so[:, 0:T], op=ALU.mult), reads=[r2r, sor], writes=[R_yh[h]])

        c0 = mk(0)
        projFI(c0); evacFI(c0); projQO(c0); E1(c0); evacQO(c0); E2(c0)
        prev = c0
        pend = None
        for n in range(1, NH + 1):
            cn = mk(n) if n < NH else None
            if cn is not None:
                projFI(cn)
            F2a(pend)
            trans(prev)
            F2b(pend)
            pend = None
            AU(prev)
            if cn is not None:
                evacFI(cn)
                projQO(cn)
            chain(prev)
            oT(prev)
            if cn is not None:
                evacQO(cn)
                E1(cn)
            F1(prev)
            pend = prev
            if cn is not None:
                E2(cn)
            prev = cn
        F2a(pend)
        F2b(pend)

    def emit_merge(T):
        nsub = T // 128
        for pr in range(8):
            gcv, gcr = wget("win", 44 + pr, 0)
            ghv, ghr = wget("win", 52 + pr, 0)
            bcv, bcr = wget("bc", pr, 0)
            bhv, bhr = wget("bh", pr, 0)
            for c in range(2):
                oc = 2 * pr + c
                bGc, bGcr = proj_fm(T, gcv, gcr, c)
                bGh, bGhr = proj_fm(T, ghv, ghr, c)
                bPc, bPcr = pf_next()
                for k in range(8):
                    P.op("pe", lambda e, bPc=bPc, k=k, c=c, bcv=bcv: e.matmul(bPc[:, 0:T], lhsT=bcv[:, k, c * 128:(c + 1) * 128], rhs=ycT(k, T),
                                                                             start=(k == 0), stop=(k == 7)), reads=[bcr, R_yc[k]], writes=[bPcr])
                bPh, bPhr = pf_next()
                for k in range(16):
                    P.op("pe", lambda e, bPh=bPh, k=k, c=c, bhv=bhv: e.matmul(bPh[:, 0:T], lhsT=bhv[:, k, c * 128:(c + 1) * 128], rhs=yhT(k, T),
                                                                             start=(k == 0), stop=(k == 15)), reads=[bhr, R_yh[k]], writes=[bPhr])
                s1, s1r = f32w[0], R_f32w[0]
                s2, s2r = f32w[1], R_f32w[1]
                P.op("act", lambda e, s1=s1, bGc=bGc: e.activation(out=s1[:, 0:T], in_=bGc[:, 0:T], func=AF.Sigmoid), reads=[bGcr], writes=[s1r])
                P.op("act", lambda e, s2=s2, bGh=bGh: e.activation(out=s2[:, 0:T], in_=bGh[:, 0:T], func=AF.Sigmoid), reads=[bGhr], writes=[s2r])
                P.op("dve", lambda e, s1=s1, bPc=bPc: e.tensor_tensor(out=s1[:, 0:T], in0=bPc[:, 0:T], in1=s1[:, 0:T], op=ALU.mult),
                     reads=[bPcr, s1r], writes=[s1r])
                P.op("dve", lambda e, s2=s2, bPh=bPh: e.tensor_tensor(out=s2[:, 0:T], in0=bPh[:, 0:T], in1=s2[:, 0:T], op=ALU.mult),
                     reads=[bPhr, s2r], writes=[s2r])
                P.op("dve", lambda e, s1=s1, s2=s2, oc=oc: e.tensor_tensor(out=mT(oc, T), in0=s1[:, 0:T], in1=s2[:, 0:T], op=ALU.add),
                     reads=[s1r, s2r], writes=[R_m[oc]])
        wm = W["wo"]
        for cg in range(4):
            banks = [pf_next() for _ in range(nsub)]
            for ks in range(wm.nks):
                wv, wr = wget("wo", cg, ks)
                for kk in range(wm.kc_count(ks)):
                    k = ks * wm.kcs + kk
                    for sub in range(nsub):
                        b, br = banks[sub]
                        P.op("pe", lambda e, b=b, wv=wv, kk=kk, k=k, sub=sub: e.matmul(b[:, :], lhsT=big[:, 24 + k, sub * 128:(sub + 1) * 128], rhs=wv[:, kk, :],
                                                                                       start=(k == 0), stop=(k == 15)), reads=[wr, R_m[k]], writes=[br])
            for sub in range(nsub):
                b, br = banks[sub]
                P.op("dve", lambda e, b=b, sub=sub, cg=cg: e.tensor_tensor(out=x_tm[:, sub, cg * 512:(cg + 1) * 512], in0=b[:, :],
                                                                           in1=x_tm[:, sub, cg * 512:(cg + 1) * 512], op=ALU.add),
                     reads=[br, R_x[sub]], writes=[R_x[sub]])

    def emit_final(T, kind, tidx, do_norm=True):
        nsub = T // 128
        for sub in range(nsub):
            if do_norm:
                hs = sub % 2
                P.op("act", lambda e, sub=sub, hs=hs: e.activation(out=hn_tm[:, hs, :], in_=x_tm[:, sub, :], func=AF.Square,
                                                                   accum_out=small[:, SM_SSQ + sub:SM_SSQ + sub + 1]),
                     reads=[R_x[sub]], writes=[R_hn[hs], R_st[sub]])
                P.op("act", lambda e, sub=sub: e.activation(out=small[:, SM_LN + sub:SM_LN + sub + 1], in_=small[:, SM_SSQ + sub:SM_SSQ + sub + 1],
                                                            func=AF.Ln, scale=1.0 / D, bias=eps_col[:, 0:1]), reads=[R_st[sub], R_eps], writes=[R_st[sub]])
                P.op("act", lambda e, sub=sub: e.activation(out=small[:, SM_RS + sub:SM_RS + sub + 1], in_=small[:, SM_LN + sub:SM_LN + sub + 1],
                                                            func=AF.Exp, scale=-0.5), reads=[R_st[sub]], writes=[R_st[sub]])
                P.op("dve", lambda e, sub=sub: e.scalar_tensor_tensor(out=x_tm[:, sub, :], in0=x_tm[:, sub, :], scalar=small[:, SM_RS + sub:SM_RS + sub + 1],
                                                                      in1=gfin[:], op0=ALU.mult, op1=ALU.mult),
                     reads=[R_x[sub], R_st[sub], R_const], writes=[R_x[sub]])
            if kind == "s":
                dst = y_s[:, :]
            else:
                r0 = tidx * TP + sub * 128
                dst = y_p[r0:r0 + 128, :]
            P.dma("act", lambda e, d=dst, sub=sub: e.dma_start(out=d, in_=x_tm[:, sub, :]), S_y[sub], reads=[R_x[sub]])

    eps_col = sb("eps_col", [128, 1], F32)
    R_eps = Res("eps")
    P.op("dve", lambda e: e.memset(eps_col[:], EPS), writes=[R_eps])

    def emit_ulast(T):
        for pr in range(4):
            cvw, cvr = wget("win", 4 + pr, 0)
            vvw, vvr = wget("win", 8 + pr, 0)
            for c in range(2):
                cc = 2 * pr + c
                bC, bCr = pf_next()
                bV, bVr = pf_next()
                for (b, br, vw, vr) in ((bC, bCr, cvw, cvr), (bV, bVr, vvw, vvr)):
                    for kc in range(KC):
                        P.op("pe", lambda e, b=b, vw=vw, kc=kc, c=c: e.matmul(b[:, 0:2], lhsT=vw[:, kc, c * 128:(c + 1) * 128], rhs=hT[:, kc, T - 2:T],
                                                                              start=(kc == 0), stop=(kc == KC - 1)), reads=[vr, R_hT], writes=[br])
                cs, csr = f32w[2], R_f32w[2]
                P.op("act", lambda e, bC=bC, cs=cs: e.copy(out=cs[:, 0:2], in_=bC[:, 0:2]), reads=[bCr], writes=[csr])
                P.op("dve", lambda e, bV=bV, cs=cs, cc=cc: e.tensor_tensor(out=xsmall[:, 16 + cc * 2:18 + cc * 2], in0=bV[:, 0:2], in1=cs[:, 0:2], op=ALU.mult),
                     reads=[bVr, csr], writes=[R_xsmall])

    def emit_tail(T, kind, tidx, segs):
        emit_norm(T, V_GM)
        emit_conv(T, segs, kind)
        emit_hgrn(T, kind)
        emit_merge(T)
        if kind == "s":
            for sq in range(2):
                P.dma("pool", lambda e, sq=sq: e.dma_start(out=conv_s_o[sq], in_=halo[:, :, sq, :]), S_cso, reads=[R_halo])
        if debug_stop == "mix":
            emit_final(T, kind, tidx, do_norm=False)
            return
        emit_norm(T, V_G2)
        emit_ffn(T, "g2", "u2", "d2")
        emit_final(T, kind, tidx)

    P.op("dve", lambda e: e.memset(xsmall[:, 0:32], 0.0), writes=[R_xsmall])
    if NPT > 0:
        emit_norm(TP, V_G1)
        emit_ffn(TP, "g1", "u1", "d1")
        emit_norm(TP, V_GM)
        emit_hgrn(TP, "p", pre=True)
        emit_ulast(TP)
    P.op("dve", lambda e: e.tensor_scalar(out=Sm[:].rearrange("p h v -> p (h v)"), in0=Sm[:].rearrange("p h v -> p (h v)"), scalar1=sel[:, 0:1],
                                          scalar2=None, op0=ALU.mult), reads=R_Sm + [R_const], writes=R_Sm)
    P.op("dve", lambda e: e.tensor_scalar(out=halo[:, :, 2, :], in0=xsmall[:, 16:32].rearrange("p (c t) -> p c t", t=2), scalar1=sel[:, 0:1],
                                          scalar2=None, op0=ALU.mult), reads=[R_xsmall, R_const, R_halo], writes=[R_halo])
    for (kind, tidx, T) in tiles[1:]:
        load_x(kind, tidx, T // 128)
        emit_norm(T, V_G1)
        emit_ffn(T, "g1", "u1", "d1")
        if debug_stop == "ffn1":
            emit_final(T, kind, tidx, do_norm=False)
        else:
            emit_tail(T, kind, tidx, [(0, T, 2, 0)])
    kind, tidx, T = tiles[0]
    load_x("s", 0, 1)
    emit_norm(T, V_G1)
    emit_ffn(T, "g1", "u1", "d1")
    if debug_stop == "ffn1":
        emit_final(T, kind, tidx, do_norm=False)
    else:
        emit_tail(T, kind, tidx, [(0, 64, 0, 0), (64, 128, 1, 66)])

    P.dma("pool", lambda e: e.dma_start(out=conv_p_o[:, :, :], in_=halo[:, :, 2, :]), S_misc_out, reads=[R_halo])
    P.dma("pool", lambda e: e.dma_start(out=st_p_o.rearrange("h k v -> k h v"), in_=Sm[:]), S_misc_out, reads=R_Sm)

    P.emit(nc, es)
    es.close()
    return nc, P


def _consts():
    ident = np.eye(128, dtype=np.float32).astype(ml_dtypes.bfloat16)
    ones = np.ones((128, 128), dtype=np.float32).astype(ml_dtypes.bfloat16)
    s = np.arange(64)[:, None]
    l = np.arange(64)[None, :]
    tri1 = (s <= l).astype(np.float32)
    tri = np.tile(tri1, (1, 8))
    rmask = np.ones((128, 512), dtype=np.float32)
    rmask[:, ::64] = 0.0
    return ident, ones, tri, rmask


def make_in_maps(inp, n_ptiles=N_PTILES):
    ident, ones, tri, rmask = _consts()
    fm = lambda v, n: np.ascontiguousarray(np.asarray(v, np.float32).reshape(n, 128).T)
    vecs = np.zeros((128, 128), np.float32)
    vecs[:, 0:16] = fm(inp["norm_ffn1"][0], 16)
    vecs[:, 16:32] = fm(inp["norm_mix"][0], 16)
    vecs[:, 32:48] = fm(inp["norm_ffn2"][0], 16)
    vecs[:, 48:64] = fm(inp["hgrn_norm"][0], 16)
    vecs[:, 64:80] = fm(inp["hgrn_lb_logits"][0], 16)
    vecs[:, 80:96] = fm(inp["hgrn_lb_logits"][1], 16)
    cw = np.asarray(inp["conv_w"][0], np.float32)
    vecs[:, 96:120] = cw.reshape(3, 8, 128).transpose(2, 1, 0).reshape(128, 24)
    gfin = np.ascontiguousarray(np.asarray(inp["norm_final"], np.float32).reshape(1, D))
    wnames = ["w_ffn1_gate", "w_ffn1_up", "w_ffn1_down", "w_ffn2_gate", "w_ffn2_up", "w_ffn2_down", "w_in", "w_br_conv",
              "w_br_hgrn", "w_out"]
    wd = {n: np.ascontiguousarray(np.asarray(inp[n], np.float32)[0]) for n in wnames}
    xp = np.asarray(inp["x_prompt"], np.float32)
    xs = np.asarray(inp["x_sample"], np.float32)
    cc = np.asarray(inp["cache_conv"], np.float32)[0]
    sh = np.asarray(inp["state_hgrn"], np.float32)[0]
    maps = []
    npt = PCH // TP
    for c in range(NCORES):
        b, q = c // 4, c % 4
        m = dict(wd)
        m["x_s"] = np.ascontiguousarray(xs[2 * c:2 * c + 2].reshape(128, D))
        m["x_p"] = np.ascontiguousarray(xp[b, q * PCH:q * PCH + npt * TP])
        m["halo_s"] = np.ascontiguousarray(cc[2 * c:2 * c + 2].reshape(2, 2, 8, 128).transpose(0, 3, 2, 1))
        m["st_s"] = np.ascontiguousarray(sh[2 * c:2 * c + 2])
        selv = np.zeros((128, 24), np.float32)
        selv[:, 0] = 1.0 if q > 0 else 0.0
        m["sel"] = selv
        m["x_w"] = np.ascontiguousarray(xp[b, q * PCH - TP:q * PCH]) if q > 0 else np.zeros((TP, D), np.float32)
        m["vecs"] = vecs
        m["gfin"] = gfin
        m["ident"] = ident
        m["ones"] = ones
        m["tri"] = tri
        m["rmask"] = rmask
        maps.append(m)
    return maps


def assemble(results, n_ptiles=N_PTILES):
    B, SEQ = 2, 16384
    y_prompt = np.zeros((B, SEQ, D), np.float32)
    y_sample = np.zeros((16, 64, D), np.float32)
    ncp = np.zeros((1, B, 2, DCONV), np.float32)
    nhp = np.zeros((1, B, NH, 128, 128), np.float32)
    ncs = np.zeros((1, 16, 2, DCONV), np.float32)
    nhs = np.zeros((1, 16, NH, 128, 128), np.float32)
    npt = PCH // TP
    for c in range(NCORES):
        r = results[c]
        b, q = c // 4, c % 4
        y_prompt[b, q * PCH:q * PCH + npt * TP] = r["y_p"]
        y_sample[2 * c:2 * c + 2] = r["y_s"].reshape(2, 64, D)
        cs = r["conv_s_o"]
        ncs[0, 2 * c:2 * c + 2] = cs.transpose(0, 3, 2, 1).reshape(2, 2, DCONV)
        nhs[0, 2 * c:2 * c + 2] = r["st_s_o"]
        if q == 3:
            ncp[0, b] = r["conv_p_o"].transpose(2, 1, 0).reshape(2, DCONV)
            nhp[0, b] = r["st_p_o"]
    return (y_prompt, y_sample, ncp, nhp, ncs, nhs)


_CACHE = {}


def kernel(**inputs):
    if "nc" not in _CACHE:
        _CACHE["nc"] = build_program()[0]
    nc = _CACHE["nc"]
    in_maps = make_in_maps(inputs)
    res = run_bass_kernel_spmd(nc, in_maps, core_ids=list(range(NCORES)))
    return assemble(res.results)
```
